# Optimizing a Trainium2 kernel written in Bass

```python
import math
import jax, jax.numpy as jnp
from jax import lax
import numpy as np

D_MODEL = 2048
BATCH = 4
SEQ = 4096
DEPTH = 1

DN_HEADS = D_MODEL // 256
DN_HEAD_DIM = 128
DN_DIM = DN_HEADS * DN_HEAD_DIM
CONV_WIDTH = 4
CHUNK = 64
DF_HEADS = D_MODEL // 512
DF_HEAD_DIM = 128
DF_DIM = DF_HEADS * 2 * DF_HEAD_DIM
Q_BLOCK = 128
NUM_BUCKETS = 32
MAX_DISTANCE = 128
D_FF = 4 * D_MODEL
IN_COLS = 4 * DN_DIM + 2 * DN_HEADS + 3 * DF_DIM
MIX_WIDTH = DN_DIM + DF_DIM

kernel_name = "hybrid_gdn_diffattn_block"


def rms_norm(x, g, eps=1e-6):
    xf = x.astype(jnp.float32)
    y = xf * lax.rsqrt(jnp.mean(xf * xf, axis=-1, keepdims=True) + eps)
    return (y * g.astype(jnp.float32)).astype(x.dtype)


def l2_norm(x, eps=1e-6):
    xf = x.astype(jnp.float32)
    return xf * lax.rsqrt(jnp.sum(xf * xf, axis=-1, keepdims=True) + eps)


def causal_dwconv(x, w):
    K, C = w.shape
    return lax.conv_general_dilated(
        x, w[:, None, :], window_strides=(1,), padding=[(K - 1, 0)],
        dimension_numbers=("NWC", "WIO", "NWC"), feature_group_count=C)


def t5_causal_bucket(n):
    max_exact = NUM_BUCKETS // 2
    n = jnp.maximum(n, 0)
    nf = jnp.maximum(n, 1).astype(jnp.float32)
    large = max_exact + (jnp.log(nf / max_exact) / math.log(MAX_DISTANCE / max_exact)
                         * (NUM_BUCKETS - max_exact)).astype(jnp.int32)
    large = jnp.minimum(large, NUM_BUCKETS - 1)
    return jnp.where(n < max_exact, n, large)


def gated_delta_rule(q, k, v, g, beta):
    B, S, H, dk = q.shape
    dv = v.shape[-1]
    C = CHUNK
    N = S // C
    f32 = jnp.float32

    def chunks(t):
        t = t.astype(f32).reshape((B, N, C, H) + t.shape[3:])
        perm = (1, 0, 3, 2) + tuple(range(4, t.ndim))
        return t.transpose(perm)

    q = chunks(q) * (dk ** -0.5)
    k = chunks(k)
    v = chunks(v)
    g = chunks(g)
    beta = chunks(beta)

    gc = jnp.cumsum(g, axis=-1)
    tril = jnp.tril(jnp.ones((C, C), dtype=bool))
    strict = jnp.tril(jnp.ones((C, C), dtype=bool), -1)
    decay = jnp.exp(jnp.where(tril, gc[..., :, None] - gc[..., None, :], -jnp.inf))

    kb = k * beta[..., None]
    vb = v * beta[..., None]
    L = jnp.where(strict, jnp.einsum("nbhid,nbhjd->nbhij", kb, k) * decay, 0.0)
    eye = jnp.eye(C, dtype=f32)
    T = lax.linalg.triangular_solve(L + eye, jnp.broadcast_to(eye, L.shape),
                                    left_side=True, lower=True, unit_diagonal=True)
    u = jnp.einsum("nbhij,nbhjd->nbhid", T, vb)
    w = jnp.einsum("nbhij,nbhjd->nbhid", T, kb * jnp.exp(gc)[..., None])
    qk = jnp.where(tril, jnp.einsum("nbhid,nbhjd->nbhij", q, k) * decay, 0.0)

    def step(state, inp):
        qi, ki, ui, wi, gci, qki = inp
        v_new = ui - jnp.einsum("bhck,bhkv->bhcv", wi, state)
        o = (jnp.einsum("bhck,bhkv->bhcv", qi * jnp.exp(gci)[..., None], state)
             + jnp.einsum("bhij,bhjv->bhiv", qki, v_new))
        glast = gci[..., -1]
        kdec = ki * jnp.exp(glast[..., None] - gci)[..., None]
        state = state * jnp.exp(glast)[..., None, None] + jnp.einsum("bhck,bhcv->bhkv", kdec, v_new)
        return state, o

    s0 = jnp.zeros((B, H, dk, dv), f32)
    _, o = lax.scan(step, s0, (q, k, u, w, gc, qk))
    return o.transpose(1, 0, 3, 2, 4).reshape(B, S, H, dv)


def diff_attention(q, k, v, lam, bias_table):
    B, S, H, _, d = q.shape
    NB = S // Q_BLOCK
    qb_all = (q * (d ** -0.5)).reshape(B, NB, Q_BLOCK, H, 2, d).transpose(1, 0, 3, 4, 2, 5)
    kt = k.transpose(0, 2, 3, 1, 4)
    vt = v.transpose(0, 2, 1, 3)
    k_pos = jnp.arange(S)

    def block(args):
        qb, i = args
        q_pos = i * Q_BLOCK + jnp.arange(Q_BLOCK)
        rel = q_pos[:, None] - k_pos[None, :]
        bias = bias_table[t5_causal_bucket(rel)].astype(jnp.float32)
        bias = bias.transpose(2, 0, 1)
        logits = jnp.einsum("bhmqd,bhmkd->bhmqk", qb, kt).astype(jnp.float32)
        logits = logits + bias[None, :, None]
        logits = jnp.where((rel >= 0)[None, None, None], logits, -jnp.inf)
        p = jax.nn.softmax(logits, axis=-1)
        attn = p[:, :, 0] - lam * p[:, :, 1]
        return jnp.einsum("bhqk,bhkv->bhqv", attn.astype(vt.dtype), vt).astype(jnp.float32)

    o = lax.map(block, (qb_all, jnp.arange(NB)))
    return o.transpose(1, 0, 3, 2, 4).reshape(B, S, H, 2 * d)


def setup_inputs(seed: int = 0) -> dict:
    key = jax.random.key(seed)
    ks = jax.random.split(key, 20)
    f32 = jnp.float32
    nrm = lambda k, shape, s: jax.random.normal(k, shape, f32) * s
    x = jax.random.normal(ks[0], (BATCH, SEQ, D_MODEL), f32)
    attn_norm = 1.0 + nrm(ks[1], (DEPTH, D_MODEL), 0.02)
    w_in = nrm(ks[2], (DEPTH, D_MODEL, IN_COLS), D_MODEL ** -0.5)
    conv_w = nrm(ks[3], (DEPTH, CONV_WIDTH, 3 * DN_DIM), CONV_WIDTH ** -0.5)
    a_log = jnp.log(jax.random.uniform(ks[4], (DEPTH, DN_HEADS), f32, 1.0, 16.0))
    dt = jnp.exp(jax.random.uniform(ks[5], (DEPTH, DN_HEADS), f32, math.log(1e-3), math.log(1e-1)))
    dt_bias = dt + jnp.log(-jnp.expm1(-dt))
    dn_norm = 1.0 + nrm(ks[6], (DEPTH, DN_HEAD_DIM), 0.02)
    lambda_q1 = nrm(ks[7], (DEPTH, DF_HEAD_DIM), 0.1)
    lambda_k1 = nrm(ks[8], (DEPTH, DF_HEAD_DIM), 0.1)
    lambda_q2 = nrm(ks[9], (DEPTH, DF_HEAD_DIM), 0.1)
    lambda_k2 = nrm(ks[10], (DEPTH, DF_HEAD_DIM), 0.1)
    df_norm = 1.0 + nrm(ks[11], (DEPTH, 2 * DF_HEAD_DIM), 0.02)
    rel_bias = nrm(ks[12], (NUM_BUCKETS, DF_HEADS), 0.2)
    w_o = nrm(ks[13], (DEPTH, MIX_WIDTH, D_MODEL), MIX_WIDTH ** -0.5)
    mlp_norm = 1.0 + nrm(ks[14], (DEPTH, D_MODEL), 0.02)
    w_up = nrm(ks[15], (DEPTH, D_MODEL, D_FF), D_MODEL ** -0.5)
    w_down = nrm(ks[16], (DEPTH, D_FF, D_MODEL), D_FF ** -0.5)
    final_norm = 1.0 + nrm(ks[17], (D_MODEL,), 0.02)
    return {"x": x, "attn_norm": attn_norm, "w_in": w_in, "conv_w": conv_w,
            "a_log": a_log, "dt_bias": dt_bias, "dn_norm": dn_norm,
            "lambda_q1": lambda_q1, "lambda_k1": lambda_k1,
            "lambda_q2": lambda_q2, "lambda_k2": lambda_k2, "df_norm": df_norm,
            "rel_bias": rel_bias, "w_o": w_o, "mlp_norm": mlp_norm,
            "w_up": w_up, "w_down": w_down, "final_norm": final_norm}


def reference(x, attn_norm, w_in, conv_w, a_log, dt_bias, dn_norm,
              lambda_q1, lambda_k1, lambda_q2, lambda_k2, df_norm,
              rel_bias, w_o, mlp_norm, w_up, w_down, final_norm):
    B, S, _ = x.shape
    f32 = jnp.float32
    sizes = [3 * DN_DIM, DN_DIM, DN_HEADS, DN_HEADS, DF_DIM, DF_DIM, DF_DIM]
    cuts = list(np.cumsum(sizes)[:-1])
    h = x
    for l in range(DEPTH):
        u = rms_norm(h, attn_norm[l])
        proj = jnp.einsum("bsd,de->bse", u, w_in[l])
        dn_qkv, dn_z, dn_b, dn_a, df_q, df_k, df_v = jnp.split(proj, cuts, axis=-1)

        qkv = jax.nn.silu(causal_dwconv(dn_qkv, conv_w[l]))
        q, k, v = jnp.split(qkv, 3, axis=-1)
        q = l2_norm(q.reshape(B, S, DN_HEADS, DN_HEAD_DIM))
        k = l2_norm(k.reshape(B, S, DN_HEADS, DN_HEAD_DIM))
        v = v.reshape(B, S, DN_HEADS, DN_HEAD_DIM)
        beta = jax.nn.sigmoid(dn_b.astype(f32))
        g = -jnp.exp(a_log[l].astype(f32)) * jax.nn.softplus(dn_a.astype(f32) + dt_bias[l].astype(f32))
        o_dn = gated_delta_rule(q, k, v, g, beta)
        z = dn_z.reshape(B, S, DN_HEADS, DN_HEAD_DIM).astype(f32)
        o_dn = rms_norm(o_dn, dn_norm[l]) * jax.nn.silu(z)
        y_dn = o_dn.reshape(B, S, DN_DIM).astype(x.dtype)

        lam_init = 0.8 - 0.6 * math.exp(-0.3 * l)
        lam = (jnp.exp(jnp.sum(lambda_q1[l].astype(f32) * lambda_k1[l].astype(f32)))
               - jnp.exp(jnp.sum(lambda_q2[l].astype(f32) * lambda_k2[l].astype(f32)))
               + lam_init)
        dq = df_q.reshape(B, S, DF_HEADS, 2, DF_HEAD_DIM)
        dk = df_k.reshape(B, S, DF_HEADS, 2, DF_HEAD_DIM)
        dv = df_v.reshape(B, S, DF_HEADS, 2 * DF_HEAD_DIM)
        o_df = diff_attention(dq, dk, dv, lam, rel_bias)
        o_df = rms_norm(o_df, df_norm[l], eps=1e-5) * (1.0 - lam_init)
        y_df = o_df.reshape(B, S, DF_DIM).astype(x.dtype)

        mix = jnp.concatenate([y_dn, y_df], axis=-1)
        h = h + jnp.einsum("bse,ed->bsd", mix, w_o[l])

        u = rms_norm(h, mlp_norm[l])
        hid = jnp.square(jax.nn.relu(jnp.einsum("bsd,df->bsf", u, w_up[l])))
        h = h + jnp.einsum("bsf,fd->bsd", hid, w_down[l])
    return rms_norm(h, final_norm)
```

```python
import numpy as np, math, os
from contextlib import ExitStack
import concourse.bass as bass
import concourse.mybir as mybir

F32 = mybir.dt.float32
BF16 = mybir.dt.bfloat16
AF = mybir.ActivationFunctionType
ALU = mybir.AluOpType

INF = 1 << 60
SELF_SYNC = True
SEM_CH = 4000


class Buf:
    def __init__(self, name, ap=None, root=None, off=0, n=INF, half=False):
        self.name = name
        self.ap = ap
        self.root = root if root is not None else self
        self.off = off
        self.n = n
        self.half = half
        if root is None:
            self.w = []
            self.r = {}
            self.excl = False
            self.last = {}

    def __getitem__(self, k):
        return self.ap[k]

    def reg(self, lo=None, hi=None):
        if lo is None:
            if self.n == INF:
                return (self.root, 0, INF)
            return (self.root, self.off, self.off + self.n)
        if self.half:
            lo, hi = lo // 2, (hi + 1) // 2
        return (self.root, self.off + lo, self.off + hi)


def _reg(x):
    if isinstance(x, Buf):
        return x.reg()
    return x


class Op:
    __slots__ = ("eng", "fn", "deps", "dma", "inc", "sig", "signo", "dcount", "idx", "dmawait")


class Prog:
    ENGS = ["pe", "act", "dve", "pool", "sp"]

    def __init__(self):
        self.ops = []
        self.dma_total = {}
        self.dma_keys = []

    def add(self, eng, fn, reads=(), writes=(), dma=None, inc=16):
        idx = len(self.ops)
        deps = {}
        engkey = eng if dma is None else ("dma", dma.name)
        rr = [_reg(r) for r in reads]
        ww = [_reg(w) for w in writes]
        for buf, lo, hi in rr:
            for (oi, l, h) in buf.w:
                if l < hi and lo < h:
                    deps[oi] = "raw"
        for buf, lo, hi in ww:
            for (oi, l, h) in buf.w:
                if l < hi and lo < h:
                    deps[oi] = "raw"
            for (k, l, h), oi in buf.r.items():
                if l < hi and lo < h and oi not in deps:
                    deps[oi] = "war"
        for buf, lo, hi in rr + ww:
            if buf.excl:
                for ek, oi in buf.last.items():
                    if ek != engkey and oi not in deps:
                        deps[oi] = "war"
                buf.last[engkey] = idx
        for buf, lo, hi in ww:
            buf.w = [a for a in buf.w if not (lo <= a[1] and a[2] <= hi)] + [(idx, lo, hi)]
            buf.r = {k: v for k, v in buf.r.items() if not (lo <= k[1] and k[2] <= hi)}
        for buf, lo, hi in rr:
            buf.r[(engkey, lo, hi)] = idx
        op = Op()
        op.eng = eng; op.fn = fn; op.deps = deps; op.dma = dma; op.inc = inc
        op.sig = False; op.signo = 0; op.dcount = 0; op.idx = idx
        op.dmawait = {}
        for d in deps:
            Pd = self.ops[d]
            if Pd.dma is not None:
                op.dmawait[Pd.dma] = self.dma_total[Pd.dma]
        if dma is not None:
            if dma not in self.dma_total:
                self.dma_total[dma] = 0
                self.dma_keys.append(dma)
            self.dma_total[dma] += inc
            op.dcount = self.dma_total[dma]
        self.ops.append(op)
        return op

    def emit(self, nc, stack):
        ops = self.ops
        for op in ops:
            for d, kind in op.deps.items():
                P = ops[d]
                if P.dma is not None:
                    continue
                if P.eng == op.eng:
                    if P.eng == "pe" or kind == "war" or not SELF_SYNC:
                        continue
                P.sig = True
        cnt = {e: 0 for e in self.ENGS}
        for op in ops:
            if op.sig:
                cnt[op.eng] += 1
                op.signo = cnt[op.eng]
        esems = {}
        for e in self.ENGS:
            n = (cnt[e] + SEM_CH - 1) // SEM_CH
            esems[e] = [stack.enter_context(nc.semaphore(f"s_{e}_{i}")) for i in range(max(n, 1))]
        dsems = {}
        for i, k in enumerate(self.dma_keys):
            dsems[k] = stack.enter_context(nc.semaphore(f"d_{i}_{k.name}"[:40]))
        per_eng = {e: [] for e in self.ENGS}
        for op in ops:
            per_eng[op.eng].append(op)
        nsem = sum(len(v) for v in esems.values()) + len(dsems)
        print(f"[prog] ops={len(ops)} sems={nsem} " + " ".join(f"{e}:{len(per_eng[e])}/{cnt[e]}" for e in self.ENGS))

        def run(e, eng):
            waited = {}
            nwait = 0
            for op in per_eng[e]:
                need = {}
                for d, kind in op.deps.items():
                    P = ops[d]
                    if P.dma is not None:
                        key = ("d", P.dma)
                        val = op.dmawait[P.dma]
                    else:
                        if not P.sig:
                            continue
                        if P.eng == e and (e == "pe" or kind == "war" or not SELF_SYNC):
                            continue
                        key = ("e", P.eng)
                        val = P.signo
                    if val > need.get(key, 0):
                        need[key] = val
                for key, val in need.items():
                    if waited.get(key, 0) >= val:
                        continue
                    waited[key] = val
                    if key[0] == "d":
                        eng.wait_ge(dsems[key[1]], val)
                    else:
                        si = (val - 1) // SEM_CH
                        eng.wait_ge(esems[key[1]][si], (val - 1) % SEM_CH + 1)
                    nwait += 1
                inst = op.fn(eng)
                if op.dma is not None:
                    inst.then_inc(dsems[op.dma], op.inc)
                elif op.sig:
                    si = (op.signo - 1) // SEM_CH
                    inst.then_inc(esems[e][si], 1)
            if e == "sp":
                for k in self.dma_keys:
                    if waited.get(("d", k), 0) < self.dma_total[k]:
                        eng.wait_ge(dsems[k], self.dma_total[k])
            return nwait

        with nc.Block() as block:
            @block.sync
            def _(eng):
                run("sp", eng)

            @block.tensor
            def _(eng):
                run("pe", eng)

            @block.scalar
            def _(eng):
                run("act", eng)

            @block.vector
            def _(eng):
                run("dve", eng)

            @block.gpsimd
            def _(eng):
                run("pool", eng)


class Arena:
    def __init__(self, nc, stack, nfloats):
        self.t = stack.enter_context(nc.sbuf_tensor("arena", [128, nfloats], F32))
        self.root = Buf("arena")
        self.n = nfloats
        self.off = 0
        self.marks = []
        self.uid = 0

    def mark(self):
        self.marks.append(self.off)

    def release(self):
        self.off = self.marks.pop()

    def alloc(self, name, shape, dtype=F32):
        free = 1
        for s in shape[1:]:
            free *= s
        nf = free if dtype == F32 else (free + 1) // 2
        assert self.off + nf <= self.n, f"arena overflow {name} {self.off}+{nf}>{self.n}"
        ap = self.t[:, self.off:self.off + nf]
        if dtype != F32:
            ap = ap.bitcast(dtype)
        if len(shape) > 2:
            names = " ".join(f"d{i}" for i in range(len(shape) - 1))
            kw = {f"d{i}": shape[i + 1] for i in range(len(shape) - 1)}
            ap = ap.rearrange(f"p ({names}) -> p {names}", **kw)
        if shape[0] < 128:
            ap = ap[0:shape[0]]
        off = self.off
        self.off += nf
        self.uid += 1
        return Buf(f"{name}#{self.uid}", ap, root=self.root, off=off, n=nf, half=(dtype != F32))


T = 4096
TB = 512
NTB = T // TB
D = 2048
KT = D // 128
TOWN = 2048
DFF = 8192
NPAR = 128
NCONST = 3 * 512 + 128
ARENA = 53000
EPS6 = 1e-6
LAM_INIT = 0.8 - 0.6 * math.exp(-0.3 * 0)


class K:
    def __init__(self, P):
        self.P = P

    def mm(self, out, lhsT, rhs, start, stop, r, w):
        self.P.add("pe", lambda e: e.matmul(out, lhsT=lhsT, rhs=rhs, start=start, stop=stop), reads=r, writes=w)

    def tr(self, out, in_, ident, r, w):
        self.P.add("pe", lambda e: e.transpose(out, in_, ident), reads=r, writes=w)

    def act(self, out, in_, func, r, w, scale=1.0, bias=0.0):
        self.P.add("act", lambda e: e.activation(out=out, in_=in_, func=func, bias=bias, scale=scale), reads=r, writes=w)

    def tt(self, eng, out, in0, in1, op, r, w):
        self.P.add(eng, lambda e: e.tensor_tensor(out=out, in0=in0, in1=in1, op=op), reads=r, writes=w)

    def ts(self, eng, out, in0, s1, op0, r, w, s2=None, op1=None):
        if op1 is None:
            self.P.add(eng, lambda e: e.tensor_scalar(out=out, in0=in0, scalar1=s1, scalar2=None, op0=op0), reads=r, writes=w)
        else:
            self.P.add(eng, lambda e: e.tensor_scalar(out=out, in0=in0, scalar1=s1, scalar2=s2, op0=op0, op1=op1), reads=r, writes=w)

    def stt(self, eng, out, in0, scalar, in1, op0, op1, r, w, tmp=None, tmpb=None):
        if eng == "pool":
            assert tmp is not None
            self.P.add("pool", lambda e: e.tensor_scalar(out=tmp, in0=in0, scalar1=scalar, scalar2=None, op0=op0), reads=r, writes=[tmpb])
            self.P.add("pool", lambda e: e.tensor_tensor(out=out, in0=tmp, in1=in1, op=op1), reads=list(r) + [tmpb], writes=w)
            return
        self.P.add(eng, lambda e: e.scalar_tensor_tensor(out=out, in0=in0, scalar=scalar, in1=in1, op0=op0, op1=op1), reads=r, writes=w)

    def cp(self, eng, out, in_, r, w):
        if eng == "act":
            self.P.add("act", lambda e: e.copy(out=out, in_=in_), reads=r, writes=w)
        else:
            self.P.add(eng, lambda e: e.tensor_copy(out=out, in_=in_), reads=r, writes=w)

    def memset(self, eng, out, val, w):
        self.P.add(eng, lambda e: e.memset(out, val), writes=w)

    def recip(self, out, in_, r, w):
        self.P.add("dve", lambda e: e.reciprocal(out=out, in_=in_), reads=r, writes=w)

    def dma(self, q, out, in_, r, w, key):
        self.P.add(q, lambda e: e.dma_start(out=out, in_=in_), reads=r, writes=w, dma=key)


def build(dbg=None):
    dbg = dbg or {}
    nc = bass.Bass("TRN2", target_bir_lowering=False)
    dt_in = lambda name, shape: nc.dram_tensor(name, shape, F32, kind="ExternalInput").ap()
    xT_d = dt_in("xT", [D, T])
    xTo_d = dt_in("xTo", [D, TOWN])
    wdn_d = dt_in("w_dn", [4, D, 514])
    wdf_d = dt_in("w_df", [2, D, 768])
    wo_d = dt_in("w_o", [D, D])
    wup_d = dt_in("w_up", [D, DFF])
    wdown_d = dt_in("w_down", [DFF, D])
    const_d = dt_in("consts", [128, NCONST])
    par_d = dt_in("par", [128, NPAR])
    bias_d = dt_in("bias_t", [2, 5, 128, 512])
    out_d = nc.dram_tensor("outT", [D, TOWN], F32, kind="ExternalOutput").ap()
    xbf_d = nc.dram_tensor("xbf_s", [128, KT, T], BF16, kind="Internal").ap()
    send_l = [nc.dram_tensor(f"send_s{i}", [1024, TB], BF16, kind="Internal").ap() for i in range(NTB)]
    recv_l = [nc.dram_tensor(f"recv_s{i}", [2048, TB], BF16, kind="Internal").ap() for i in range(NTB)]
    wo_s = nc.dram_tensor("wo_s", [8, 128, 16, 256], BF16, kind="Internal").ap()
    wup_s = nc.dram_tensor("wup_s", [32, 128, 16, 256], BF16, kind="Internal").ap()
    wdn_s = nc.dram_tensor("wdn_s", [8, 4, 128, 16, 256], BF16, kind="Internal").ap()
    dbg_out = {}
    for name, shape in dbg.get("outs", {}).items():
        dbg_out[name] = nc.dram_tensor(name, shape, F32, kind="ExternalOutput").ap()

    P = Prog()
    k = K(P)
    st = ExitStack()
    ar = Arena(nc, st, ARENA)
    PS = [Buf(f"ps{i}", st.enter_context(nc.psum_tensor(f"ps{i}", [128, 512], F32))) for i in range(8)]
    for b_ in PS:
        b_.excl = True
    psi = [0]
    psr = [list(range(8))]

    def psn():
        lst = psr[0]
        b = PS[lst[psi[0] % len(lst)]]
        psi[0] += 1
        return b

    XBF = Buf("xbf_dram")
    SENDB = [Buf(f"send_dram{i}") for i in range(NTB)]
    RECVB = [Buf(f"recv_dram{i}") for i in range(NTB)]
    WOS = Buf("wo_s_dram"); WUPS = Buf("wup_s_dram"); WDNS = Buf("wdn_s_dram"); OUTD = Buf("out_dram")
    DBG = Buf("dbg_dram")

    CON = ar.alloc("consts", [128, NCONST])
    PAR = ar.alloc("par", [128, NPAR])
    ONESB = ar.alloc("ones_bf", [128, 128], BF16)
    k.dma("sp", CON[:, :], const_d[:, :], [], [CON], CON)
    k.dma("sp", PAR[:, :], par_d[:, :], [], [PAR], PAR)
    strict4 = CON[:, 0:512]
    lowT4 = CON[:, 512:1024]
    ident4 = CON[:, 1024:1536]
    ones = CON[:, 1536:1664]
    SLm = CON[:, 0:128]
    Um = CON[:, 512:640]
    ident = CON[:, 1024:1152]
    k.cp("dve", ONESB[:, :], ones, [CON], [ONESB])
    c_attn = lambda kt: PAR[:, kt:kt + 1]
    c_mlp = lambda kt: PAR[:, 16 + kt:17 + kt]
    c_fin = lambda kt: PAR[:, 32 + kt:33 + kt]
    c_conv = lambda h, s, j: PAR[:, 48 + (h * 3 + s) * 4 + j:48 + (h * 3 + s) * 4 + j + 1]
    c_alog = PAR[:, 96:100]
    c_dtb = lambda h: PAR[:, 100 + h:101 + h]
    c_dnw = PAR[:, 104:105]
    c_lam = PAR[:, 105:109]
    c_dfw = PAR[:, 109:111]
    c_c31 = lambda h: PAR[:, 111 + h:112 + h]
    c_sel = lambda i: PAR[:, 113 + i:114 + i]
    SM = ar.alloc("small", [128, 32])
    k.act(SM[:, 0:4], c_alog, AF.Exp, [PAR], [SM.reg(0, 4)])
    k.ts("dve", SM[:, 0:4], SM[:, 0:4], -1.0, ALU.mult, [SM.reg(0, 4)], [SM.reg(0, 4)])
    k.ts("dve", SM[:, 4:5], c_dnw, 0.5, ALU.mult, [PAR], [SM.reg(4, 5)])
    k.ts("dve", SM[:, 6:8], c_dfw, 1.0 - LAM_INIT, ALU.mult, [PAR], [SM.reg(6, 8)])
    k.tt("dve", SM[:, 10:11], PAR[:, 105:106], PAR[:, 106:107], ALU.mult, [PAR], [SM.reg(10, 11)])
    k.tt("dve", SM[:, 11:12], PAR[:, 107:108], PAR[:, 108:109], ALU.mult, [PAR], [SM.reg(11, 12)])
    pl = psn()
    k.mm(pl[:, 0:2], ones, SM[:, 10:12], True, True, [CON, SM.reg(10, 12)], [pl])
    k.act(SM[:, 12:14], pl[:, 0:2], AF.Exp, [pl], [SM.reg(12, 14)])
    k.tt("dve", SM[:, 5:6], SM[:, 12:13], SM[:, 13:14], ALU.subtract, [SM.reg(12, 14)], [SM.reg(5, 6)])
    k.ts("dve", SM[:, 5:6], SM[:, 5:6], LAM_INIT, ALU.add, [SM.reg(5, 6)], [SM.reg(5, 6)])
    k.ts("dve", SM[:, 8:9], SM[:, 5:6], -1.0, ALU.mult, [SM.reg(5, 6)], [SM.reg(8, 9)])

    ar.mark()
    RSTD = ar.alloc("rstd", [128, T])

    NST = 2
    STG = [ar.alloc(f"stg{i}", [128, 2056]) for i in range(NST)]
    stg_i = [0]

    def stg():
        b = STG[stg_i[0] % NST]
        stg_i[0] += 1
        return b

    CBF = [ar.alloc(f"cbf{i}", [128, 2048], BF16) for i in range(2)]
    cbf_i = [0]

    def cbf():
        b = CBF[cbf_i[0] % 2]
        cbf_i[0] += 1
        return b

    cast_rr = [0]

    def precast_steps():
        steps = []
        for kt in range(16):
            def f(kt=kt):
                s = stg(); c = cbf()
                k.dma("sp", s[:, 0:2048], wo_d[kt * 128:(kt + 1) * 128, :], [], [s], s)
                k.cp("pool", c[:, :], s[:, 0:2048], [s], [c])
                k.dma("pool", wo_s[:, :, kt, :].rearrange("a p c -> p a c"),
                      c[:, :].rearrange("p (a c) -> p a c", a=8), [c], [WOS.reg(kt, kt + 1)], c)
            steps.append(f)
        for kt in range(16):
            for cc in range(4):
                def f(kt=kt, cc=cc):
                    s = stg(); c = cbf()
                    k.dma("sp", s[:, 0:2048], wup_d[kt * 128:(kt + 1) * 128, cc * 2048:(cc + 1) * 2048], [], [s], s)
                    k.ts("pool", c[:, :], s[:, 0:2048], c_mlp(kt), ALU.mult, [s, PAR], [c])
                    k.dma("pool", wup_s[cc * 8:(cc + 1) * 8, :, kt, :].rearrange("a p c -> p a c"),
                          c[:, :].rearrange("p (a c) -> p a c", a=8), [c], [WUPS.reg(kt * 4 + cc, kt * 4 + cc + 1)], c)
                steps.append(f)
        for ft in range(64):
            def f(ft=ft):
                s = stg(); c = cbf()
                k.dma("sp", s[:, 0:2048], wdown_d[ft * 128:(ft + 1) * 128, :], [], [s], s)
                k.cp("pool", c[:, :], s[:, 0:2048], [s], [c])
                k.dma("pool", wdn_s[:, ft // 16, :, ft % 16, :].rearrange("a p c -> p a c"),
                      c[:, :].rearrange("p (a c) -> p a c", a=8), [c], [WDNS.reg(ft, ft + 1)], c)
            steps.append(f)
        return steps

    PSTEPS = precast_steps()
    pstep_i = [0]

    def pump(n):
        if dbg.get("no_precast"):
            return
        for _ in range(n):
            if pstep_i[0] < len(PSTEPS):
                PSTEPS[pstep_i[0]]()
                pstep_i[0] += 1

    def load_w(dst, src, ncols):
        g = 2056 // ncols
        g = 4 if g >= 4 else 2
        srcv = src.rearrange("(kt p) c -> p kt c", p=128)
        for k0 in range(0, KT, g):
            s = stg()
            k.dma("sp", s[:, 0:g * ncols].rearrange("p (a c) -> p a c", a=g), srcv[:, k0:k0 + g, :], [], [s], s)
            for j in range(g):
                kt = k0 + j
                k.ts("pool", dst[:, kt, :], s[:, j * ncols:(j + 1) * ncols], c_attn(kt), ALU.mult,
                     [s, PAR], [dst.reg(kt * ncols, (kt + 1) * ncols)])

    def rsqrt(out, in_, scale, eps, tmp, r, w):
        k.act(tmp, in_, AF.Ln, r, [tmp_b[0]], scale=scale, bias=eps)
        k.act(out, tmp, AF.Exp, [tmp_b[0]], w, scale=-0.5)

    tmp_b = [None]

    dbg_n = [0]

    def dbg_dump(name, src_ap, src_buf, sl=None):
        if name in dbg_out:
            dst = dbg_out[name] if sl is None else dbg_out[name][sl]
            dbg_n[0] += 1
            k.dma("sp", dst, src_ap, [src_buf], [DBG.reg(dbg_n[0], dbg_n[0] + 1)], Buf(f"dbgk{dbg_n[0]}"))

    def rsq(out_ap, in_ap, scale, eps, tbuf, tmp_ap, r, w):
        k.act(tmp_ap, in_ap, AF.Ln, r, [tbuf], scale=scale, bias=eps)
        k.act(out_ap, tmp_ap, AF.Exp, [tbuf], w, scale=-0.5)

    xT_v = xT_d.rearrange("(kt p) t -> p kt t", p=128)

    ar.mark()
    SQ0 = [ar.alloc(f"sq0_{i}", [128, 512], BF16) for i in range(3)]
    LNT = ar.alloc("lnt", [128, 512])
    sqi = 0
    castq = 0
    for tb in range(NTB):
        ssb = psn()
        for kg in range(4):
            s = stg()
            k.dma("sp", s[:, 0:2048].rearrange("p (a c) -> p a c", a=4),
                  xT_v[:, kg * 4:(kg + 1) * 4, tb * TB:(tb + 1) * TB], [], [s], s)
            for j in range(4):
                kt = kg * 4 + j
                sq = SQ0[sqi % 3]; sqi += 1
                k.act(sq[:, :], s[:, j * 512:(j + 1) * 512], AF.Square, [s], [sq])
                k.mm(ssb[:, :], ONESB[:, :], sq[:, :], kt == 0, kt == KT - 1, [ONESB, sq], [ssb])
            c = cbf()
            k.cp("dve" if castq % 2 == 0 else "pool", c[:, :], s[:, 0:2048], [s], [c]); castq += 1
            k.dma("pool", xbf_d[:, kg * 4:(kg + 1) * 4, tb * TB:(tb + 1) * TB],
                  c[:, :].rearrange("p (a c) -> p a c", a=4), [c], [XBF.reg(tb * 4 + kg, tb * 4 + kg + 1)], c)
        rsq(RSTD[:, tb * TB:(tb + 1) * TB], ssb[:, :], 1.0 / D, EPS6, LNT, LNT[:, :], [ssb],
            [RSTD.reg(tb * TB, (tb + 1) * TB)])
    ar.release()
    psr[0] = list(range(8))
    dbg_dump("d_rstd", RSTD[0:1, :], RSTD)
    if dbg.get("stop") == "s0":
        return finish(nc, P, st)

    ar.mark()
    WDN = ar.alloc("wdn", [128, KT, 514], BF16)
    XB = [ar.alloc(f"xb{i}", [128, KT, TB], BF16) for i in range(2)]
    PJ = [ar.alloc(f"pj{s}", [128, 515]) for s in range(3)]
    PZ = ar.alloc("pz", [128, 512])
    BA = ar.alloc("ba", [128, 512])
    CV = [ar.alloc(f"cv{s}", [128, 512]) for s in range(3)]
    TA = ar.alloc("ta", [128, 512])
    TP = ar.alloc("tp", [128, 512])
    SZ2 = ar.alloc("sz2", [128, 512])
    SQB = ar.alloc("sqb", [128, 512], BF16)
    RI = ar.alloc("ri", [128, 512])
    QN = ar.alloc("qn", [128, 512]); KN = ar.alloc("kn", [128, 512])
    KBG = ar.alloc("kbg", [128, 512]); KDEC = ar.alloc("kdec", [128, 512]); VBt = ar.alloc("vb", [128, 512])
    GSL = ar.alloc("gsl", [128, 512]); GU = ar.alloc("gu", [128, 512])
    EE = ar.alloc("ee", [128, 512]); ET = ar.alloc("et", [128, 512]); EGCB = ar.alloc("egcb", [128, 512])
    QG = ar.alloc("qg", [128, 512]); QKT = ar.alloc("qkt", [128, 512])
    XX = [ar.alloc(f"xx{i}", [128, 512]) for i in range(2)]
    XXT = [ar.alloc(f"xxt{i}", [128, 512]) for i in range(2)]
    RR = ar.alloc("rr", [128, 512])
    UU = ar.alloc("uu", [128, 512]); WT = ar.alloc("wt", [128, 512])
    VN = [ar.alloc(f"vn{i}", [128, 128]) for i in range(2)]
    SS = ar.alloc("state", [128, 128])
    OO = ar.alloc("oo", [128, 512])
    YB = [ar.alloc(f"yb{i}", [128, 512], BF16) for i in range(2)]
    GT = ar.alloc("gates", [128, 96])
    g_ = lambda a, b: GT[:, a:b]
    gr = lambda a, b: GT.reg(a, b)
    QSCALE = 0.5 * (128 ** -0.5)
    blk = 0
    psr[0] = list(range(7))
    for h in range(0 if dbg.get("skip_dn") else 4):
        load_w(WDN, wdn_d[h], 514)
        k.memset("pool", SS[:, :], 0.0, [SS])
        for s in range(3):
            k.memset("pool", PJ[s][:, 0:3], 0.0, [PJ[s].reg(0, 3)])
        for tb in range(NTB):
            pump(dbg.get("pump_dn", 2))
            xb = XB[blk % 2]; blk += 1
            k.dma("sp", xb[:, :, :], xbf_d[:, :, tb * TB:(tb + 1) * TB], [XBF.reg(tb * 4, tb * 4 + 4)], [xb], xb)
            rblk = RSTD[:, tb * TB:(tb + 1) * TB]
            rreg = RSTD.reg(tb * TB, (tb + 1) * TB)
            for ct in range(4):
                ps = psn()
                for kt in range(KT):
                    k.mm(ps[:, :], WDN[:, kt, ct * 128:(ct + 1) * 128], xb[:, kt, :], kt == 0, kt == KT - 1, [WDN, xb], [ps])
                if ct < 3:
                    k.tt("dve", PJ[ct][:, 3:515], ps[:, :], rblk, ALU.mult, [ps, rreg], [PJ[ct].reg(3, 515)])
                else:
                    k.tt("dve", PZ[:, :], ps[:, :], rblk, ALU.mult, [ps, rreg], [PZ])
            ps = psn()
            for kt in range(KT):
                k.mm(ps[0:2, :], WDN[:, kt, 512:514], xb[:, kt, :], kt == 0, kt == KT - 1, [WDN, xb], [ps])
            k.tt("dve", BA[0:2, :], ps[0:2, :], RSTD[0:2, tb * TB:(tb + 1) * TB], ALU.mult, [ps, rreg], [BA])
            if dbg.get("stop") == "dnA":
                return finish(nc, P, st)
            psg = psn()
            for c in range(4):
                k.mm(psg[:, c * 2:c * 2 + 2], BA[0:2, c * 128:(c + 1) * 128], ident[0:2, 0:2], True, True, [BA, CON], [psg.reg(c * 2, c * 2 + 2)])
            k.cp("dve", g_(0, 8), psg[:, 0:8], [psg], [gr(0, 8)])
            raw = GT[:, 0:8].rearrange("p (c t) -> p c t", t=2)
            bcol = raw[:, :, 0]
            acol = raw[:, :, 1]
            if dbg.get("stop") == "dnB1":
                return finish(nc, P, st)
            k.act(g_(8, 12), bcol, AF.Exp, [gr(0, 8)], [gr(8, 12)], scale=-1.0)
            k.ts("dve", g_(8, 12), g_(8, 12), 1.0, ALU.add, [gr(8, 12)], [gr(8, 12)])
            k.recip(g_(8, 12), g_(8, 12), [gr(8, 12)], [gr(8, 12)])
            if dbg.get("stop") == "dnB2":
                return finish(nc, P, st)
            k.ts("dve", g_(12, 16), acol, c_dtb(h), ALU.add, [gr(0, 8), PAR], [gr(12, 16)])
            k.act(g_(16, 20), g_(12, 16), AF.Abs, [gr(12, 16)], [gr(16, 20)])
            k.act(g_(20, 24), g_(16, 20), AF.Exp, [gr(16, 20)], [gr(20, 24)], scale=-1.0)
            k.act(g_(20, 24), g_(20, 24), AF.Ln, [gr(20, 24)], [gr(20, 24)], bias=1.0)
            k.ts("dve", g_(24, 28), g_(12, 16), 0.0, ALU.max, [gr(12, 16)], [gr(24, 28)])
            k.tt("dve", g_(24, 28), g_(24, 28), g_(20, 24), ALU.add, [gr(24, 28), gr(20, 24)], [gr(24, 28)])
            k.ts("dve", g_(28, 32), g_(24, 28), SM[:, h:h + 1], ALU.mult, [gr(24, 28), SM.reg(0, 4)], [gr(28, 32)])
            if dbg.get("stop") == "dnB3":
                return finish(nc, P, st)
            for c in range(4):
                cs = slice(c * 128, (c + 1) * 128)
                k.ts("pool", GSL[:, cs], SLm, g_(28 + c, 29 + c), ALU.mult, [CON, gr(28, 32)], [GSL.reg(c * 128, (c + 1) * 128)])
                k.ts("pool", GU[:, cs], Um, g_(28 + c, 29 + c), ALU.mult, [CON, gr(28, 32)], [GU.reg(c * 128, (c + 1) * 128)])
            psGC = psn(); psGCC = psn()
            k.mm(psGC[:, :], ones, GU[:, :], True, True, [CON, GU], [psGC])
            k.mm(psGCC[:, :], Um, GU[:, :], True, True, [CON, GU], [psGCC])
            gccol = psGCC[:, :].rearrange("p (c i) -> p c i", i=128)[:, :, 127]
            glbc = psGC[:, :].rearrange("p (c i) -> p c i", i=128)[:, :, 127]
            k.cp("dve", g_(32, 36), gccol, [psGCC], [gr(32, 36)])
            k.act(g_(36, 40), gccol, AF.Exp, [psGCC], [gr(36, 40)])
            k.tt("dve", g_(40, 44), glbc, g_(32, 36), ALU.subtract, [psGC, gr(32, 36)], [gr(40, 44)])
            k.act(g_(44, 48), g_(40, 44), AF.Exp, [gr(40, 44)], [gr(44, 48)])
            k.act(g_(48, 52), glbc, AF.Exp, [psGC], [gr(48, 52)])
            k.act(EGCB[:, :], psGC[:, :], AF.Exp, [psGC], [EGCB])
            if dbg.get("stop") == "dnB4":
                return finish(nc, P, st)
            k.tt("dve", g_(52, 56), g_(8, 12), g_(36, 40), ALU.mult, [gr(8, 12), gr(36, 40)], [gr(52, 56)])
            k.ts("dve", g_(56, 60), g_(8, 12), 0.5, ALU.mult, [gr(8, 12)], [gr(56, 60)])
            k.ts("dve", g_(60, 64), g_(8, 12), -1.0, ALU.mult, [gr(8, 12)], [gr(60, 64)])
            if dbg.get("stop") == "dnB":
                return finish(nc, P, st)
            for s in range(3):
                eng = "dve" if s != 1 else "pool"
                pj = PJ[s]; cv = CV[s]
                k.ts(eng, cv[:, :], pj[:, 0:512], c_conv(h, s, 0), ALU.mult, [pj, PAR], [cv])
                for j in range(1, 4):
                    k.stt(eng, cv[:, :], pj[:, j:j + 512], c_conv(h, s, j), cv[:, :], ALU.mult, ALU.add, [pj, PAR, cv], [cv],
                          tmp=TP[:, :], tmpb=TP)
                k.cp("pool", pj[:, 0:3], pj[:, 512:515], [pj.reg(512, 515)], [pj.reg(0, 3)])
                k.act(TA[:, :], cv[:, :], AF.Tanh, [cv], [TA], scale=0.5)
                k.stt("dve", cv[:, :], TA[:, :], 1.0, cv[:, :], ALU.add, ALU.mult, [TA, cv], [cv])
            k.act(TA[:, :], PZ[:, :], AF.Tanh, [PZ], [TA], scale=0.5)
            k.stt("dve", SZ2[:, :], TA[:, :], 1.0, PZ[:, :], ALU.add, ALU.mult, [TA, PZ], [SZ2])
            if dbg.get("stop") == "dnC":
                return finish(nc, P, st)
            for s, dst, sc in ((0, QN, QSCALE), (1, KN, 0.5)):
                k.act(SQB[:, :], CV[s][:, :], AF.Square, [CV[s]], [SQB])
                ps = psn()
                k.mm(ps[:, :], ONESB[:, :], SQB[:, :], True, True, [ONESB, SQB], [ps])
                rsq(RI[:, :], ps[:, :], 0.25, EPS6, TA, TA[:, :], [ps], [RI])
                k.stt("dve", dst[:, :], CV[s][:, :], sc, RI[:, :], ALU.mult, ALU.mult, [CV[s], RI], [dst])
            if dbg.get("stop") == "dnD":
                return finish(nc, P, st)
            psk = psn(); psv = psn()
            for c in range(4):
                cs = slice(c * 128, (c + 1) * 128)
                k.tr(psk[:, cs], KN[:, cs], ident, [KN, CON], [psk.reg(c * 128, (c + 1) * 128)])
                k.tr(psv[:, cs], CV[2][:, cs], ident, [CV[2], CON], [psv.reg(c * 128, (c + 1) * 128)])
            for c in range(4):
                cs = slice(c * 128, (c + 1) * 128)
                k.ts("dve", KBG[:, cs], psk[:, cs], g_(52 + c, 53 + c), ALU.mult, [psk, gr(52, 56)], [KBG.reg(c * 128, (c + 1) * 128)])
                k.ts("dve", KDEC[:, cs], psk[:, cs], g_(44 + c, 45 + c), ALU.mult, [psk, gr(44, 48)], [KDEC.reg(c * 128, (c + 1) * 128)])
                k.ts("dve", VBt[:, cs], psv[:, cs], g_(56 + c, 57 + c), ALU.mult, [psv, gr(56, 60)], [VBt.reg(c * 128, (c + 1) * 128)])
            if dbg.get("stop") == "dnE":
                return finish(nc, P, st)
            psD = psn(); psDT = psn()
            k.mm(psD[:, :], Um, GSL[:, :], True, True, [CON, GSL], [psD])
            k.mm(psDT[:, :], SLm, GU[:, :], True, True, [CON, GU], [psDT])
            k.act(EE[:, :], psD[:, :], AF.Exp, [psD], [EE])
            k.tt("pool", EE[:, :], EE[:, :], strict4, ALU.mult, [EE, CON], [EE])
            k.act(ET[:, :], psDT[:, :], AF.Exp, [psDT], [ET])
            k.tt("pool", ET[:, :], ET[:, :], lowT4, ALU.mult, [ET, CON], [ET])
            k.tt("pool", QG[:, :], QN[:, :], EGCB[:, :], ALU.mult, [QN, EGCB], [QG])
            if dbg.get("stop") == "dnF":
                return finish(nc, P, st)
            psKK = psn(); psKQ = psn()
            for c in range(4):
                cs = slice(c * 128, (c + 1) * 128)
                k.mm(psKK[:, cs], KN[:, cs], KN[:, cs], True, True, [KN], [psKK.reg(c * 128, (c + 1) * 128)])
                k.mm(psKQ[:, cs], KN[:, cs], QN[:, cs], True, True, [KN, QN], [psKQ.reg(c * 128, (c + 1) * 128)])
            X0 = XX[0]; XT0 = XXT[0]
            for c in range(4):
                cs = slice(c * 128, (c + 1) * 128)
                k.stt("dve", X0[:, cs], psKK[:, cs], g_(60 + c, 61 + c), EE[:, cs], ALU.mult, ALU.mult,
                      [psKK, gr(60, 64), EE], [X0.reg(c * 128, (c + 1) * 128)])
            k.tt("dve", QKT[:, :], psKQ[:, :], ET[:, :], ALU.mult, [psKQ, ET], [QKT])
            if dbg.get("stop") == "dnG":
                return finish(nc, P, st)
            psT = psn()
            for c in range(4):
                cs = slice(c * 128, (c + 1) * 128)
                k.tr(psT[:, cs], X0[:, cs], ident, [X0, CON], [psT.reg(c * 128, (c + 1) * 128)])
            k.cp("act", XT0[:, :], psT[:, :], [psT], [XT0])
            k.tt("dve", RR[:, :], psT[:, :], ident4, ALU.add, [psT, CON], [RR])
            if dbg.get("stop") == "dnG2":
                return finish(nc, P, st)
            cur = 0
            for n in range(1, 7):
                Xc, XTc = XX[cur], XXT[cur]
                Xn, XTn = XX[1 - cur], XXT[1 - cur]
                psX = psn()
                for c in range(4):
                    cs = slice(c * 128, (c + 1) * 128)
                    k.mm(psX[:, cs], XTc[:, cs], Xc[:, cs], True, True, [XTc, Xc], [psX.reg(c * 128, (c + 1) * 128)])
                if n < 6:
                    psXT = psn()
                    for c in range(4):
                        cs = slice(c * 128, (c + 1) * 128)
                        k.mm(psXT[:, cs], Xc[:, cs], XTc[:, cs], True, True, [XTc, Xc], [psXT.reg(c * 128, (c + 1) * 128)])
                k.cp("act", Xn[:, :], psX[:, :], [psX], [Xn])
                if n < 6:
                    k.cp("dve", XTn[:, :], psXT[:, :], [psXT], [XTn])
                psR = psn()
                for c in range(4):
                    cs = slice(c * 128, (c + 1) * 128)
                    k.mm(psR[:, cs], Xn[:, cs], RR[:, cs], True, True, [Xn, RR], [psR.reg(c * 128, (c + 1) * 128)])
                k.tt("dve", RR[:, :], psR[:, :], RR[:, :], ALU.add, [psR, RR], [RR])
                cur = 1 - cur
                if dbg.get("stop") == "dnG3" and n == 1:
                    return finish(nc, P, st)
            if dbg.get("stop") == "dnH":
                return finish(nc, P, st)
            psU = psn(); psW = psn()
            for c in range(4):
                cs = slice(c * 128, (c + 1) * 128)
                k.mm(psU[:, cs], RR[:, cs], VBt[:, cs], True, True, [RR, VBt], [psU.reg(c * 128, (c + 1) * 128)])
                k.mm(psW[:, cs], KBG[:, cs], RR[:, cs], True, True, [RR, KBG], [psW.reg(c * 128, (c + 1) * 128)])
            k.cp("act", UU[:, :], psU[:, :], [psU], [UU])
            k.cp("dve", WT[:, :], psW[:, :], [psW], [WT])
            if dbg.get("stop") == "dnI":
                return finish(nc, P, st)
            psO = PS[7]
            for c in range(4):
                cs = slice(c * 128, (c + 1) * 128)
                creg = (c * 128, (c + 1) * 128)
                ps1 = psn()
                k.mm(ps1[:, 0:128], WT[:, cs], SS[:, :], True, True, [WT, SS], [ps1])
                k.mm(psO[:, cs], SS[:, :], QG[:, cs], True, False, [SS, QG], [psO.reg(*creg)])
                vn = VN[c % 2]
                k.tt("dve", vn[:, :], UU[:, cs], ps1[:, 0:128], ALU.subtract, [UU, ps1], [vn])
                k.mm(psO[:, cs], vn[:, :], QKT[:, cs], False, True, [vn, QKT], [psO.reg(*creg)])
                ps3 = psn()
                k.mm(ps3[:, 0:128], KDEC[:, cs], vn[:, :], True, True, [KDEC, vn], [ps3])
                k.stt("dve", SS[:, :], SS[:, :], g_(48 + c, 49 + c), ps3[:, 0:128], ALU.mult, ALU.add, [SS, gr(48, 52), ps3], [SS])
            if dbg.get("stop") == "dnJ":
                return finish(nc, P, st)
            k.cp("act", OO[:, :], psO[:, :], [psO], [OO])
            k.act(SQB[:, :], psO[:, :], AF.Square, [psO], [SQB])
            ps = psn()
            k.mm(ps[:, :], ONESB[:, :], SQB[:, :], True, True, [ONESB, SQB], [ps])
            rsq(RI[:, :], ps[:, :], 1.0 / 128, EPS6, TA, TA[:, :], [ps], [RI])
            k.tt("pool", OO[:, :], OO[:, :], RI[:, :], ALU.mult, [OO, RI], [OO])
            yb = YB[tb % 2]
            k.stt("dve", yb[:, :], OO[:, :], SM[:, 4:5], SZ2[:, :], ALU.mult, ALU.mult, [OO, SM.reg(4, 5), SZ2], [yb])
            k.dma("sp", send_l[tb][h * 128:(h + 1) * 128, :], yb[:, :], [yb], [SENDB[tb].reg(h, h + 1)], yb)
            if h == dbg.get("dh", 0) and tb == dbg.get("dtb", 0):
                dbg_dump("d_qn", QN[:, :], QN); dbg_dump("d_kn", KN[:, :], KN); dbg_dump("d_v2", CV[2][:, :], CV[2])
                dbg_dump("d_gates", GT[:, 0:64], GT); dbg_dump("d_tt", RR[:, :], RR); dbg_dump("d_oo", OO[:, :], OO)
                dbg_dump("d_uu", UU[:, :], UU); dbg_dump("d_wt", WT[:, :], WT); dbg_dump("d_qkt", QKT[:, :], QKT)
            if dbg.get("stop") == "dn00" or (dbg.get("stop") == "dn0" and tb == NTB - 1):
                return finish(nc, P, st)
    ar.release()
    psr[0] = list(range(8))

    rg = [[0, 1], [2, 3], [4, 5], [6, 7]]

    def exchange(tb):
        src = send_l[tb]; dst = recv_l[tb]
        P.add("pool", lambda e: e.collective_compute("AllGather", ALU.bypass, replica_groups=rg, ins=[src], outs=[dst]),
              reads=[SENDB[tb]], writes=[RECVB[tb]], dma=RECVB[tb], inc=1)

    ar.mark()
    WDF = ar.alloc("wdf", [128, KT, 768], BF16)
    XB = [ar.alloc(f"xbd{i}", [128, KT, TB], BF16) for i in range(2)]
    KC = [ar.alloc(f"kc{m}", [128, T], BF16) for m in range(2)]
    VC = ar.alloc("vc", [128, 32, 256], BF16)
    BT = ar.alloc("bt", [128, 5, 512])
    QB = [ar.alloc(f"qb{m}", [128, 512], BF16) for m in range(2)]
    RCOL = ar.alloc("rcol", [128, 4])
    PT = [ar.alloc(f"pt{i}", [128, 512], BF16) for i in range(4)]
    LG = [ar.alloc(f"lg{m}", [128, 512]) for m in range(2)]
    RL = [ar.alloc(f"rl{m}", [128, 512]) for m in range(2)]
    T1 = ar.alloc("t1", [128, 512])
    OD = [ar.alloc(f"od{i}", [128, 512]) for i in range(2)]
    SQD = [ar.alloc(f"sqd{i}", [128, 512], BF16) for i in range(2)]
    RI = ar.alloc("rid", [128, 512]); TA = ar.alloc("tad", [128, 512])
    YD = [ar.alloc(f"yd{i}", [128, 512], BF16) for i in range(4)]
    ASCALE = 128 ** -0.5
    ydi = 0
    blk = 0
    for hl in range(2):
        load_w(WDF, wdf_d[hl], 768)
        k.dma("sp", BT[:, :, :], bias_d[hl].rearrange("t p c -> p t c"), [], [BT], BT)
        for tb in range(NTB):
            pump(dbg.get("pump_df", 8))
            xb = XB[blk % 2]; blk += 1
            k.dma("sp", xb[:, :, :], xbf_d[:, :, tb * TB:(tb + 1) * TB], [XBF.reg(tb * 4, tb * 4 + 4)], [xb], xb)
            rblk = RSTD[:, tb * TB:(tb + 1) * TB]
            rreg = RSTD.reg(tb * TB, (tb + 1) * TB)
            for ct in range(4):
                ps = psn()
                for kt in range(KT):
                    k.mm(ps[:, :], WDF[:, kt, ct * 128:(ct + 1) * 128], xb[:, kt, :], kt == 0, kt == KT - 1, [WDF, xb], [ps])
                if ct < 2:
                    k.tt("dve", QB[ct][:, :], ps[:, :], rblk, ALU.mult, [ps, rreg], [QB[ct]])
                else:
                    m = ct - 2
                    k.tt("dve", KC[m][:, tb * TB:(tb + 1) * TB], ps[:, :], rblk, ALU.mult, [ps, rreg], [KC[m].reg(tb * TB, (tb + 1) * TB)])
            psc = psn()
            for t4 in range(4):
                k.mm(psc[:, t4:t4 + 1], RSTD[0:1, tb * TB + t4 * 128:tb * TB + (t4 + 1) * 128], ones[0:1, 0:1], True, True,
                     [rreg, CON], [psc.reg(t4, t4 + 1)])
            k.cp("dve", RCOL[:, :], psc[:, 0:4], [psc], [RCOL])
            for t4 in range(4):
                ps = psn()
                for kt in range(KT):
                    k.mm(ps[:, 0:256], xb[:, kt, t4 * 128:(t4 + 1) * 128], WDF[:, kt, 512:768], kt == 0, kt == KT - 1, [WDF, xb], [ps])
                tile_i = tb * 4 + t4
                k.ts("dve", VC[:, tile_i, :], ps[:, 0:256], RCOL[:, t4:t4 + 1], ALU.mult, [ps, RCOL], [VC.reg(tile_i * 256, (tile_i + 1) * 256)])
            nkt = 4 * tb + 4
            ACC = [[PS[m * 3 + j] for j in range(3)] for m in range(2)]
            SB_ = [PS[6], PS[7]]

            def s_and_e(kt, m):
                rel = tb * TB - kt * 128
                k.mm(SB_[m][:, :], KC[m][:, kt * 128:(kt + 1) * 128], QB[m][:, :], True, True,
                     [KC[m].reg(kt * 128, (kt + 1) * 128), QB[m]], [SB_[m]])
                pt = PT[(kt * 2 + m) % 4]
                if rel >= 256:
                    k.act(pt[:, :], SB_[m][:, :], AF.Exp, [SB_[m], PAR], [pt], scale=ASCALE, bias=c_c31(hl))
                else:
                    typ = 0 if rel == 128 else 1 + (-rel) // 128
                    k.stt("dve", LG[m][:, :], SB_[m][:, :], ASCALE, BT[:, typ, :], ALU.mult, ALU.add, [SB_[m], BT], [LG[m]])
                    k.act(pt[:, :], LG[m][:, :], AF.Exp, [LG[m]], [pt])

            def pv(kt, m):
                pt = PT[(kt * 2 + m) % 4]
                st_, sp_ = (kt == 0), (kt == nkt - 1)
                for dvc in range(2):
                    k.mm(ACC[m][dvc][:, :], VC[:, kt, dvc * 128:(dvc + 1) * 128], pt[:, :], st_, sp_,
                         [VC.reg(kt * 256, (kt + 1) * 256), pt], [ACC[m][dvc]])
                k.mm(ACC[m][2][:, :], ONESB[:, :], pt[:, :], st_, sp_, [ONESB, pt], [ACC[m][2]])

            s_and_e(0, 0); s_and_e(0, 1)
            for kt in range(nkt):
                if kt + 1 < nkt:
                    s_and_e(kt + 1, 0)
                pv(kt, 0)
                if kt + 1 < nkt:
                    s_and_e(kt + 1, 1)
                pv(kt, 1)
            for m in range(2):
                k.recip(RL[m][:, :], ACC[m][2][:, :], [ACC[m][2]], [RL[m]])
            k.ts("pool", RL[1][:, :], RL[1][:, :], SM[:, 8:9], ALU.mult, [RL[1], SM.reg(8, 9)], [RL[1]])
            for dvc in range(2):
                k.tt("dve", T1[:, :], ACC[0][dvc][:, :], RL[0][:, :], ALU.mult, [ACC[0][dvc], RL[0]], [T1])
                k.tt("dve", OD[dvc][:, :], ACC[1][dvc][:, :], RL[1][:, :], ALU.mult, [ACC[1][dvc], RL[1]], [OD[dvc]])
                k.tt("pool", OD[dvc][:, :], OD[dvc][:, :], T1[:, :], ALU.add, [OD[dvc], T1], [OD[dvc]])
            ps = PS[6]
            for dvc in range(2):
                k.act(SQD[dvc][:, :], OD[dvc][:, :], AF.Square, [OD[dvc]], [SQD[dvc]])
                k.mm(ps[:, :], ONESB[:, :], SQD[dvc][:, :], dvc == 0, dvc == 1, [ONESB, SQD[dvc]], [ps])
            rsq(RI[:, :], ps[:, :], 1.0 / 256, 1e-5, TA, TA[:, :], [ps], [RI])
            for dvc in range(2):
                yb = YD[ydi % 4]; ydi += 1
                k.stt("dve", yb[:, :], OD[dvc][:, :], SM[:, 6 + dvc:7 + dvc], RI[:, :], ALU.mult, ALU.mult,
                      [OD[dvc], SM.reg(6, 8), RI], [yb])
                row = (4 + hl * 2 + dvc) * 128
                rid = 4 + hl * 2 + dvc
                k.dma("sp", send_l[tb][row:row + 128, :], yb[:, :], [yb], [SENDB[tb].reg(rid, rid + 1)], yb)
            if hl == 1 and not dbg.get("no_cc"):
                exchange(tb)
            if hl == dbg.get("dh", 0) and tb == dbg.get("dtb", 1):
                dbg_dump("d_od0", OD[0][:, :], OD[0]); dbg_dump("d_od1", OD[1][:, :], OD[1])
            if dbg.get("stop") == "df01" and tb == dbg.get("dtb", 1):
                return finish(nc, P, st)
    ar.release()
    pump(10 ** 6)
    ar.release()
    if dbg.get("stop") == "mix":
        return finish(nc, P, st)

    MIX = ar.alloc("mix", [128, KT, TB], BF16)
    H1 = ar.alloc("h1", [128, KT, TB])
    U2 = ar.alloc("u2", [128, KT, TB], BF16)
    HID = ar.alloc("hid", [128, 64, TB], BF16)
    WSL = [ar.alloc(f"wsl{i}", [128, 16, 256], BF16) for i in range(4)]
    RA = [ar.alloc(f"ra{i}", [128, 512], BF16) for i in range(2)]
    RBb = [ar.alloc(f"rb{i}", [128, 512], BF16) for i in range(2)]
    SQP = [ar.alloc(f"sqp{i}", [128, 512], BF16) for i in range(2)]
    RI = ar.alloc("ri3", [128, 512]); TA = ar.alloc("ta3", [128, 512])
    RT = [ar.alloc(f"rt{i}", [128, 512]) for i in range(2)]
    OB = [ar.alloc(f"ob{i}", [128, 512]) for i in range(2)]
    wsi = [0]

    def wslot():
        b = WSL[wsi[0] % 4]
        wsi[0] += 1
        return b

    xTo_v = xTo_d.rearrange("(kt p) t -> p kt t", p=128)
    sq_i = 0
    for ob in range(TOWN // TB):
        tsl = slice(ob * TB, (ob + 1) * TB)
        for kg in range(4):
            k.dma("sp", H1[:, kg * 4:(kg + 1) * 4, :], xTo_v[:, kg * 4:(kg + 1) * 4, tsl], [], [H1.reg(kg * 2048, (kg + 1) * 2048)], H1)
        for rt in range(16):
            ra = RA[rt % 2]; rb = RBb[rt % 2]
            k.dma("sp", ra[:, :], recv_l[ob][rt * 128:(rt + 1) * 128, :], [RECVB[ob]], [ra], ra)
            k.dma("sp", rb[:, :], recv_l[4 + ob][rt * 128:(rt + 1) * 128, :], [RECVB[4 + ob]], [rb], rb)
            mreg = MIX.reg(rt * TB, (rt + 1) * TB)
            k.ts("pool", MIX[:, rt, :], ra[:, :], c_sel(0), ALU.mult, [ra, PAR], [mreg])
            k.stt("dve", MIX[:, rt, :], rb[:, :], c_sel(1), MIX[:, rt, :], ALU.mult, ALU.add, [rb, PAR, mreg], [mreg])
        for dt2 in range(8):
            sl = wslot()
            k.dma("sp", sl[:, :, :], wo_s[dt2], [WOS], [sl], sl)
            for half in range(2):
                dt = dt2 * 2 + half
                ps = psn()
                for kt in range(KT):
                    k.mm(ps[:, :], sl[:, kt, half * 128:(half + 1) * 128], MIX[:, kt, :], kt == 0, kt == KT - 1, [sl, MIX], [ps])
                hreg = H1.reg(dt * TB, (dt + 1) * TB)
                k.tt("dve", H1[:, dt, :], ps[:, :], H1[:, dt, :], ALU.add, [ps, hreg], [hreg])
        if ob == 0:
            dbg_dump("d_h1", H1[:, 0, :], H1)
        ps = psn()
        for dt in range(KT):
            sq = SQP[sq_i % 2]; sq_i += 1
            k.act(sq[:, :], H1[:, dt, :], AF.Square, [H1.reg(dt * TB, (dt + 1) * TB)], [sq])
            k.mm(ps[:, :], ONESB[:, :], sq[:, :], dt == 0, dt == KT - 1, [ONESB, sq], [ps])
        rsq(RI[:, :], ps[:, :], 1.0 / D, EPS6, TA, TA[:, :], [ps], [RI])
        for dt in range(KT):
            k.tt("dve" if dt % 2 == 0 else "pool", U2[:, dt, :], H1[:, dt, :], RI[:, :], ALU.mult,
                 [H1.reg(dt * TB, (dt + 1) * TB), RI], [U2.reg(dt * TB, (dt + 1) * TB)])
        for ft2 in range(32):
            sl = wslot()
            k.dma("sp", sl[:, :, :], wup_s[ft2], [WUPS], [sl], sl)
            for half in range(2):
                ft = ft2 * 2 + half
                ps = psn()
                for kt in range(KT):
                    k.mm(ps[:, :], sl[:, kt, half * 128:(half + 1) * 128], U2[:, kt, :], kt == 0, kt == KT - 1, [sl, U2], [ps])
                rt_ = RT[ft % 2]
                k.act(rt_[:, :], ps[:, :], AF.Relu, [ps], [rt_])
                k.tt("pool" if ft % 4 != 3 else "dve", HID[:, ft, :], rt_[:, :], rt_[:, :], ALU.mult, [rt_], [HID.reg(ft * TB, (ft + 1) * TB)])
        for dt2 in range(8):
            psA = psn(); psB = psn()
            pss = [psA, psB]
            for fg in range(4):
                sl = wslot()
                k.dma("sp", sl[:, :, :], wdn_s[dt2, fg], [WDNS], [sl], sl)
                for half in range(2):
                    for f in range(16):
                        ft = fg * 16 + f
                        k.mm(pss[half][:, :], sl[:, f, half * 128:(half + 1) * 128], HID[:, ft, :], fg == 0 and f == 0, fg == 3 and f == 15,
                             [sl, HID.reg(ft * TB, (ft + 1) * TB)], [pss[half]])
            for half in range(2):
                dt = dt2 * 2 + half
                hreg = H1.reg(dt * TB, (dt + 1) * TB)
                k.tt("dve", H1[:, dt, :], pss[half][:, :], H1[:, dt, :], ALU.add, [pss[half], hreg], [hreg])
        ps = psn()
        for dt in range(KT):
            sq = SQP[sq_i % 2]; sq_i += 1
            k.act(sq[:, :], H1[:, dt, :], AF.Square, [H1.reg(dt * TB, (dt + 1) * TB)], [sq])
            k.mm(ps[:, :], ONESB[:, :], sq[:, :], dt == 0, dt == KT - 1, [ONESB, sq], [ps])
        rsq(RI[:, :], ps[:, :], 1.0 / D, EPS6, TA, TA[:, :], [ps], [RI])
        for dt in range(KT):
            ob_ = OB[dt % 2]
            k.stt("dve", ob_[:, :], H1[:, dt, :], c_fin(dt), RI[:, :], ALU.mult, ALU.mult,
                  [H1.reg(dt * TB, (dt + 1) * TB), PAR, RI], [ob_])
            k.dma("sp", out_d[dt * 128:(dt + 1) * 128, tsl], ob_[:, :], [ob_], [OUTD.reg(ob * 16 + dt, ob * 16 + dt + 1)], ob_)
    return finish(nc, P, st)


def finish(nc, P, st):
    P.emit(nc, st)
    st.close()
    return nc


def _bucket(n):
    nf = np.maximum(n, 1).astype(np.float32)
    large = 16 + (np.log(nf / np.float32(16)) / np.float32(math.log(128 / 16)) * np.float32(16)).astype(np.int32)
    large = np.minimum(large, 31)
    return np.where(n < 16, n, large)


def _consts():
    i = np.arange(128)
    strict = (i[:, None] > i[None, :]).astype(np.float32)
    lowT = (i[:, None] <= i[None, :]).astype(np.float32)
    ident = np.eye(128, dtype=np.float32)
    c = np.concatenate([np.tile(strict, (1, 4)), np.tile(lowT, (1, 4)), np.tile(ident, (1, 4)),
                        np.ones((128, 128), np.float32)], axis=1)
    return np.ascontiguousarray(c)


def prep_inputs(inputs, cores=range(8)):
    x = np.asarray(inputs["x"], np.float32)
    w_in = np.asarray(inputs["w_in"], np.float32)[0]
    conv_w = np.asarray(inputs["conv_w"], np.float32)[0]
    a_log = np.asarray(inputs["a_log"], np.float32)[0]
    dt_bias = np.asarray(inputs["dt_bias"], np.float32)[0]
    rel_bias = np.asarray(inputs["rel_bias"], np.float32)
    w_o = np.asarray(inputs["w_o"], np.float32)[0]
    w_up = np.ascontiguousarray(np.asarray(inputs["w_up"], np.float32)[0])
    w_down = np.ascontiguousarray(np.asarray(inputs["w_down"], np.float32)[0])
    consts = _consts()
    p = np.arange(128)
    perm = []
    for r in range(2):
        for hh_ in range(4):
            perm.append((r * 4 + hh_) * 128 + p)
        for hl in range(2):
            for dvc in range(2):
                perm.append(1024 + (r * 2 + hl) * 256 + dvc * 128 + p)
    perm = np.concatenate(perm)
    w_o_perm = np.ascontiguousarray(w_o[perm, :])
    qq = np.arange(512)[None, :]
    kk = np.arange(128)[:, None]
    xT_cache = {}
    maps = []
    for c in cores:
        b, hh = c // 2, c % 2
        if b not in xT_cache:
            xT_cache[b] = np.ascontiguousarray(x[b].T)
        xT = xT_cache[b]
        xTo = np.ascontiguousarray(xT[:, hh * TOWN:(hh + 1) * TOWN])
        w_dn = np.empty((4, D, 514), np.float32)
        for h in range(4):
            H = hh * 4 + h
            for s in range(4):
                w_dn[h, :, s * 128:(s + 1) * 128] = w_in[:, s * 1024 + H * 128:s * 1024 + (H + 1) * 128]
            w_dn[h, :, 512] = w_in[:, 4096 + H]
            w_dn[h, :, 513] = w_in[:, 4104 + H]
        w_df = np.empty((2, D, 768), np.float32)
        for hl in range(2):
            Hf = hh * 2 + hl
            base = 4112
            w_df[hl, :, 0:256] = w_in[:, base + Hf * 256:base + (Hf + 1) * 256]
            w_df[hl, :, 256:512] = w_in[:, base + 1024 + Hf * 256:base + 1024 + (Hf + 1) * 256]
            w_df[hl, :, 512:768] = w_in[:, base + 2048 + Hf * 256:base + 2048 + (Hf + 1) * 256]
        par = np.zeros((128, NPAR), np.float32)
        par[:, 0:16] = np.asarray(inputs["attn_norm"], np.float32)[0].reshape(16, 128).T
        par[:, 16:32] = np.asarray(inputs["mlp_norm"], np.float32)[0].reshape(16, 128).T
        par[:, 32:48] = np.asarray(inputs["final_norm"], np.float32).reshape(16, 128).T
        for h in range(4):
            H = hh * 4 + h
            for s in range(3):
                for j in range(4):
                    par[:, 48 + (h * 3 + s) * 4 + j] = conv_w[j, s * 1024 + H * 128:s * 1024 + (H + 1) * 128]
            par[:, 96 + h] = a_log[H]
            par[:, 100 + h] = dt_bias[H]
        par[:, 104] = np.asarray(inputs["dn_norm"], np.float32)[0]
        par[:, 105] = np.asarray(inputs["lambda_q1"], np.float32)[0]
        par[:, 106] = np.asarray(inputs["lambda_k1"], np.float32)[0]
        par[:, 107] = np.asarray(inputs["lambda_q2"], np.float32)[0]
        par[:, 108] = np.asarray(inputs["lambda_k2"], np.float32)[0]
        par[:, 109:111] = np.asarray(inputs["df_norm"], np.float32)[0].reshape(2, 128).T
        bias_t = np.empty((2, 5, 128, 512), np.float32)
        for hl in range(2):
            Hf = hh * 2 + hl
            par[:, 111 + hl] = rel_bias[31, Hf]
            for typ in range(5):
                n = (qq + 128 - kk) if typ == 0 else (qq - kk - 128 * (typ - 1))
                tbl = rel_bias[_bucket(np.maximum(n, 0)), Hf]
                bias_t[hl, typ] = np.where(n >= 0, tbl, np.float32(-30000.0))
        par[:, 113] = 1.0 if hh == 0 else 0.0
        par[:, 114] = 0.0 if hh == 0 else 1.0
        maps.append({"xT": xT, "xTo": xTo, "w_dn": w_dn, "w_df": w_df, "w_o": w_o_perm, "w_up": w_up, "w_down": w_down,
                     "consts": consts, "par": par, "bias_t": bias_t})
    return maps


_NC_CACHE = {}


def kernel(**inputs):
    from concourse.bass_utils import run_bass_kernel_spmd
    if "nc" not in _NC_CACHE:
        _NC_CACHE["nc"] = build()
    nc = _NC_CACHE["nc"]
    maps = prep_inputs(inputs)
    res = run_bass_kernel_spmd(nc, maps, core_ids=list(range(8)))
    out = np.empty((4, T, D), np.float32)
    for c in range(8):
        b, hh = c // 2, c % 2
        out[b, hh * TOWN:(hh + 1) * TOWN, :] = np.asarray(res.results[c]["outT"]).T
    return out
```

```python
import numpy as np, math, os
from contextlib import ExitStack
import concourse.bass as bass
import concourse.mybir as mybir

F32 = mybir.dt.float32
BF16 = mybir.dt.bfloat16
AF = mybir.ActivationFunctionType
ALU = mybir.AluOpType

INF = 1 << 60
SELF_SYNC = True
SEM_CH = 4000


class Buf:
    def __init__(self, name, ap=None, root=None, off=0, n=INF, half=False):
        self.name = name
        self.ap = ap
        self.root = root if root is not None else self
        self.off = off
        self.n = n
        self.half = half
        if root is None:
            self.w = []
            self.r = {}
            self.excl = False
            self.last = {}

    def __getitem__(self, k):
        return self.ap[k]

    def reg(self, lo=None, hi=None):
        if lo is None:
            if self.n == INF:
                return (self.root, 0, INF)
            return (self.root, self.off, self.off + self.n)
        if self.half:
            lo, hi = lo // 2, (hi + 1) // 2
        return (self.root, self.off + lo, self.off + hi)


def _reg(x):
    if isinstance(x, Buf):
        return x.reg()
    return x


class Op:
    __slots__ = ("eng", "fn", "deps", "dma", "inc", "sig", "signo", "dcount", "idx", "dmawait")


class Prog:
    ENGS = ["pe", "act", "dve", "pool", "sp"]

    def __init__(self):
        self.ops = []
        self.dma_total = {}
        self.dma_keys = []

    def add(self, eng, fn, reads=(), writes=(), dma=None, inc=16):
        idx = len(self.ops)
        deps = {}
        engkey = eng if dma is None else ("dma", dma.name)
        rr = [_reg(r) for r in reads]
        ww = [_reg(w) for w in writes]
        for buf, lo, hi in rr:
            for (oi, l, h) in buf.w:
                if l < hi and lo < h:
                    deps[oi] = "raw"
        for buf, lo, hi in ww:
            for (oi, l, h) in buf.w:
                if l < hi and lo < h:
                    deps[oi] = "raw"
            for (k, l, h), oi in buf.r.items():
                if l < hi and lo < h and oi not in deps:
                    deps[oi] = "war"
        for buf, lo, hi in rr + ww:
            if buf.excl:
                for ek, oi in buf.last.items():
                    if ek != engkey and oi not in deps:
                        deps[oi] = "war"
                buf.last[engkey] = idx
        for buf, lo, hi in ww:
            buf.w = [a for a in buf.w if not (lo <= a[1] and a[2] <= hi)] + [(idx, lo, hi)]
            buf.r = {k: v for k, v in buf.r.items() if not (lo <= k[1] and k[2] <= hi)}
        for buf, lo, hi in rr:
            buf.r[(engkey, lo, hi)] = idx
        op = Op()
        op.eng = eng; op.fn = fn; op.deps = deps; op.dma = dma; op.inc = inc
        op.sig = False; op.signo = 0; op.dcount = 0; op.idx = idx
        op.dmawait = {}
        for d in deps:
            Pd = self.ops[d]
            if Pd.dma is not None:
                op.dmawait[Pd.dma] = self.dma_total[Pd.dma]
        if dma is not None:
            if dma not in self.dma_total:
                self.dma_total[dma] = 0
                self.dma_keys.append(dma)
            self.dma_total[dma] += inc
            op.dcount = self.dma_total[dma]
        self.ops.append(op)
        return op

    def emit(self, nc, stack):
        ops = self.ops
        for op in ops:
            for d, kind in op.deps.items():
                P = ops[d]
                if P.dma is not None:
                    continue
                if P.eng == op.eng:
                    if P.eng == "pe" or kind == "war" or not SELF_SYNC:
                        continue
                P.sig = True
        cnt = {e: 0 for e in self.ENGS}
        for op in ops:
            if op.sig:
                cnt[op.eng] += 1
                op.signo = cnt[op.eng]
        esems = {}
        for e in self.ENGS:
            n = (cnt[e] + SEM_CH - 1) // SEM_CH
            esems[e] = [stack.enter_context(nc.semaphore(f"s_{e}_{i}")) for i in range(max(n, 1))]
        dsems = {}
        for i, k in enumerate(self.dma_keys):
            dsems[k] = stack.enter_context(nc.semaphore(f"d_{i}_{k.name}"[:40]))
        per_eng = {e: [] for e in self.ENGS}
        for op in ops:
            per_eng[op.eng].append(op)
        nsem = sum(len(v) for v in esems.values()) + len(dsems)
        print(f"[prog] ops={len(ops)} sems={nsem} " + " ".join(f"{e}:{len(per_eng[e])}/{cnt[e]}" for e in self.ENGS))

        def run(e, eng):
            waited = {}
            nwait = 0
            for op in per_eng[e]:
                need = {}
                for d, kind in op.deps.items():
                    P = ops[d]
                    if P.dma is not None:
                        key = ("d", P.dma)
                        val = op.dmawait[P.dma]
                    else:
                        if not P.sig:
                            continue
                        if P.eng == e and (e == "pe" or kind == "war" or not SELF_SYNC):
                            continue
                        key = ("e", P.eng)
                        val = P.signo
                    if val > need.get(key, 0):
                        need[key] = val
                for key, val in need.items():
                    if waited.get(key, 0) >= val:
                        continue
                    waited[key] = val
                    if key[0] == "d":
                        eng.wait_ge(dsems[key[1]], val)
                    else:
                        si = (val - 1) // SEM_CH
                        eng.wait_ge(esems[key[1]][si], (val - 1) % SEM_CH + 1)
                    nwait += 1
                inst = op.fn(eng)
                if op.dma is not None:
                    inst.then_inc(dsems[op.dma], op.inc)
                elif op.sig:
                    si = (op.signo - 1) // SEM_CH
                    inst.then_inc(esems[e][si], 1)
            if e == "sp":
                for k in self.dma_keys:
                    if waited.get(("d", k), 0) < self.dma_total[k]:
                        eng.wait_ge(dsems[k], self.dma_total[k])
            return nwait

        with nc.Block() as block:
            @block.sync
            def _(eng):
                run("sp", eng)

            @block.tensor
            def _(eng):
                run("pe", eng)

            @block.scalar
            def _(eng):
                run("act", eng)

            @block.vector
            def _(eng):
                run("dve", eng)

            @block.gpsimd
            def _(eng):
                run("pool", eng)


class Arena:
    def __init__(self, nc, stack, nfloats):
        self.t = stack.enter_context(nc.sbuf_tensor("arena", [128, nfloats], F32))
        self.root = Buf("arena")
        self.n = nfloats
        self.off = 0
        self.marks = []
        self.uid = 0

    def mark(self):
        self.marks.append(self.off)

    def release(self):
        self.off = self.marks.pop()

    def alloc(self, name, shape, dtype=F32):
        free = 1
        for s in shape[1:]:
            free *= s
        nf = free if dtype == F32 else (free + 1) // 2
        assert self.off + nf <= self.n, f"arena overflow {name} {self.off}+{nf}>{self.n}"
        ap = self.t[:, self.off:self.off + nf]
        if dtype != F32:
            ap = ap.bitcast(dtype)
        if len(shape) > 2:
            names = " ".join(f"d{i}" for i in range(len(shape) - 1))
            kw = {f"d{i}": shape[i + 1] for i in range(len(shape) - 1)}
            ap = ap.rearrange(f"p ({names}) -> p {names}", **kw)
        if shape[0] < 128:
            ap = ap[0:shape[0]]
        off = self.off
        self.off += nf
        self.uid += 1
        return Buf(f"{name}#{self.uid}", ap, root=self.root, off=off, n=nf, half=(dtype != F32))


T = 4096
TB = 512
NTB = T // TB
D = 2048
KT = D // 128
TOWN = 2048
DFF = 8192
NPAR = 128
NCONST = 3 * 512 + 128
ARENA = 53000
EPS6 = 1e-6
LAM_INIT = 0.8 - 0.6 * math.exp(-0.3 * 0)


class K:
    def __init__(self, P):
        self.P = P

    def mm(self, out, lhsT, rhs, start, stop, r, w):
        self.P.add("pe", lambda e: e.matmul(out, lhsT=lhsT, rhs=rhs, start=start, stop=stop), reads=r, writes=w)

    def tr(self, out, in_, ident, r, w):
        self.P.add("pe", lambda e: e.transpose(out, in_, ident), reads=r, writes=w)

    def act(self, out, in_, func, r, w, scale=1.0, bias=0.0):
        self.P.add("act", lambda e: e.activation(out=out, in_=in_, func=func, bias=bias, scale=scale), reads=r, writes=w)

    def tt(self, eng, out, in0, in1, op, r, w):
        self.P.add(eng, lambda e: e.tensor_tensor(out=out, in0=in0, in1=in1, op=op), reads=r, writes=w)

    def ts(self, eng, out, in0, s1, op0, r, w, s2=None, op1=None):
        if op1 is None:
            self.P.add(eng, lambda e: e.tensor_scalar(out=out, in0=in0, scalar1=s1, scalar2=None, op0=op0), reads=r, writes=w)
        else:
            self.P.add(eng, lambda e: e.tensor_scalar(out=out, in0=in0, scalar1=s1, scalar2=s2, op0=op0, op1=op1), reads=r, writes=w)

    def stt(self, eng, out, in0, scalar, in1, op0, op1, r, w, tmp=None, tmpb=None):
        if eng == "pool":
            assert tmp is not None
            self.P.add("pool", lambda e: e.tensor_scalar(out=tmp, in0=in0, scalar1=scalar, scalar2=None, op0=op0), reads=r, writes=[tmpb])
            self.P.add("pool", lambda e: e.tensor_tensor(out=out, in0=tmp, in1=in1, op=op1), reads=list(r) + [tmpb], writes=w)
            return
        self.P.add(eng, lambda e: e.scalar_tensor_tensor(out=out, in0=in0, scalar=scalar, in1=in1, op0=op0, op1=op1), reads=r, writes=w)

    def cp(self, eng, out, in_, r, w):
        if eng == "act":
            self.P.add("act", lambda e: e.copy(out=out, in_=in_), reads=r, writes=w)
        else:
            self.P.add(eng, lambda e: e.tensor_copy(out=out, in_=in_), reads=r, writes=w)

    def memset(self, eng, out, val, w):
        self.P.add(eng, lambda e: e.memset(out, val), writes=w)

    def recip(self, out, in_, r, w):
        self.P.add("dve", lambda e: e.reciprocal(out=out, in_=in_), reads=r, writes=w)

    def dma(self, q, out, in_, r, w, key):
        self.P.add(q, lambda e: e.dma_start(out=out, in_=in_), reads=r, writes=w, dma=key)


def build(dbg=None):
    dbg = dbg or {}
    nc = bass.Bass("TRN2", target_bir_lowering=False)
    dt_in = lambda name, shape: nc.dram_tensor(name, shape, F32, kind="ExternalInput").ap()
    xT_d = dt_in("xT", [D, T])
    xTo_d = dt_in("xTo", [D, TOWN])
    wdn_d = dt_in("w_dn", [4, D, 514])
    wdf_d = dt_in("w_df", [2, D, 768])
    wo_d = dt_in("w_o", [D, D])
    wup_d = dt_in("w_up", [D, DFF])
    wdown_d = dt_in("w_down", [DFF, D])
    const_d = dt_in("consts", [128, NCONST])
    par_d = dt_in("par", [128, NPAR])
    bias_d = dt_in("bias_t", [2, 5, 128, 512])
    out_d = nc.dram_tensor("outT", [D, TOWN], F32, kind="ExternalOutput").ap()
    xbf_d = nc.dram_tensor("xbf_s", [128, KT, T], BF16, kind="Internal").ap()
    send_l = [nc.dram_tensor(f"send_s{i}", [1024, TB], BF16, kind="Internal").ap() for i in range(NTB)]
    recv_l = [nc.dram_tensor(f"recv_s{i}", [2048, TB], BF16, kind="Internal").ap() for i in range(NTB)]
    wo_s = nc.dram_tensor("wo_s", [8, 128, 16, 256], BF16, kind="Internal").ap()
    wup_s = nc.dram_tensor("wup_s", [32, 128, 16, 256], BF16, kind="Internal").ap()
    wdn_s = nc.dram_tensor("wdn_s", [8, 4, 128, 16, 256], BF16, kind="Internal").ap()
    dbg_out = {}
    for name, shape in dbg.get("outs", {}).items():
        dbg_out[name] = nc.dram_tensor(name, shape, F32, kind="ExternalOutput").ap()

    P = Prog()
    k = K(P)
    st = ExitStack()
    ar = Arena(nc, st, ARENA)
    PS = [Buf(f"ps{i}", st.enter_context(nc.psum_tensor(f"ps{i}", [128, 512], F32))) for i in range(8)]
    for b_ in PS:
        b_.excl = True
    psi = [0]
    psr = [list(range(8))]

    def psn():
        lst = psr[0]
        b = PS[lst[psi[0] % len(lst)]]
        psi[0] += 1
        return b

    XBF = Buf("xbf_dram")
    SENDB = [Buf(f"send_dram{i}") for i in range(NTB)]
    RECVB = [Buf(f"recv_dram{i}") for i in range(NTB)]
    WOS = Buf("wo_s_dram"); WUPS = Buf("wup_s_dram"); WDNS = Buf("wdn_s_dram"); OUTD = Buf("out_dram")
    DBG = Buf("dbg_dram")

    CON = ar.alloc("consts", [128, NCONST])
    PAR = ar.alloc("par", [128, NPAR])
    ONESB = ar.alloc("ones_bf", [128, 128], BF16)
    k.dma("sp", CON[:, :], const_d[:, :], [], [CON], CON)
    k.dma("sp", PAR[:, :], par_d[:, :], [], [PAR], PAR)
    strict4 = CON[:, 0:512]
    lowT4 = CON[:, 512:1024]
    ident4 = CON[:, 1024:1536]
    ones = CON[:, 1536:1664]
    SLm = CON[:, 0:128]
    Um = CON[:, 512:640]
    ident = CON[:, 1024:1152]
    k.cp("dve", ONESB[:, :], ones, [CON], [ONESB])
    c_attn = lambda kt: PAR[:, kt:kt + 1]
    c_mlp = lambda kt: PAR[:, 16 + kt:17 + kt]
    c_fin = lambda kt: PAR[:, 32 + kt:33 + kt]
    c_conv = lambda h, s, j: PAR[:, 48 + (h * 3 + s) * 4 + j:48 + (h * 3 + s) * 4 + j + 1]
    c_alog = PAR[:, 96:100]
    c_dtb = lambda h: PAR[:, 100 + h:101 + h]
    c_dnw = PAR[:, 104:105]
    c_lam = PAR[:, 105:109]
    c_dfw = PAR[:, 109:111]
    c_c31 = lambda h: PAR[:, 111 + h:112 + h]
    c_sel = lambda i: PAR[:, 113 + i:114 + i]
    SM = ar.alloc("small", [128, 32])
    k.act(SM[:, 0:4], c_alog, AF.Exp, [PAR], [SM.reg(0, 4)])
    k.ts("dve", SM[:, 0:4], SM[:, 0:4], -1.0, ALU.mult, [SM.reg(0, 4)], [SM.reg(0, 4)])
    k.ts("dve", SM[:, 4:5], c_dnw, 0.5, ALU.mult, [PAR], [SM.reg(4, 5)])
    k.ts("dve", SM[:, 6:8], c_dfw, 1.0 - LAM_INIT, ALU.mult, [PAR], [SM.reg(6, 8)])
    k.tt("dve", SM[:, 10:11], PAR[:, 105:106], PAR[:, 106:107], ALU.mult, [PAR], [SM.reg(10, 11)])
    k.tt("dve", SM[:, 11:12], PAR[:, 107:108], PAR[:, 108:109], ALU.mult, [PAR], [SM.reg(11, 12)])
    pl = psn()
    k.mm(pl[:, 0:2], ones, SM[:, 10:12], True, True, [CON, SM.reg(10, 12)], [pl])
    k.act(SM[:, 12:14], pl[:, 0:2], AF.Exp, [pl], [SM.reg(12, 14)])
    k.tt("dve", SM[:, 5:6], SM[:, 12:13], SM[:, 13:14], ALU.subtract, [SM.reg(12, 14)], [SM.reg(5, 6)])
    k.ts("dve", SM[:, 5:6], SM[:, 5:6], LAM_INIT, ALU.add, [SM.reg(5, 6)], [SM.reg(5, 6)])
    k.ts("dve", SM[:, 8:9], SM[:, 5:6], -1.0, ALU.mult, [SM.reg(5, 6)], [SM.reg(8, 9)])

    ar.mark()
    RSTD = ar.alloc("rstd", [128, T])

    NST = 2
    STG = [ar.alloc(f"stg{i}", [128, 2056]) for i in range(NST)]
    stg_i = [0]

    def stg():
        b = STG[stg_i[0] % NST]
        stg_i[0] += 1
        return b

    CBF = [ar.alloc(f"cbf{i}", [128, 2048], BF16) for i in range(2)]
    cbf_i = [0]

    def cbf():
        b = CBF[cbf_i[0] % 2]
        cbf_i[0] += 1
        return b

    cast_rr = [0]

    def precast_steps():
        steps = []
        for kt in range(16):
            def f(kt=kt):
                s = stg(); c = cbf()
                k.dma("sp", s[:, 0:2048], wo_d[kt * 128:(kt + 1) * 128, :], [], [s], s)
                k.cp("act" if kt % 2 == 0 else "pool", c[:, :], s[:, 0:2048], [s], [c])
                k.dma("pool", wo_s[:, :, kt, :].rearrange("a p c -> p a c"),
                      c[:, :].rearrange("p (a c) -> p a c", a=8), [c], [WOS.reg(kt, kt + 1)], c)
            steps.append(f)
        for kt in range(16):
            for cc in range(4):
                def f(kt=kt, cc=cc):
                    s = stg(); c = cbf()
                    k.dma("sp", s[:, 0:2048], wup_d[kt * 128:(kt + 1) * 128, cc * 2048:(cc + 1) * 2048], [], [s], s)
                    k.ts("dve", c[:, :], s[:, 0:2048], c_mlp(kt), ALU.mult, [s, PAR], [c])
                    k.dma("pool", wup_s[cc * 8:(cc + 1) * 8, :, kt, :].rearrange("a p c -> p a c"),
                          c[:, :].rearrange("p (a c) -> p a c", a=8), [c], [WUPS.reg(kt * 4 + cc, kt * 4 + cc + 1)], c)
                steps.append(f)
        for ft in range(64):
            def f(ft=ft):
                s = stg(); c = cbf()
                k.dma("sp", s[:, 0:2048], wdown_d[ft * 128:(ft + 1) * 128, :], [], [s], s)
                k.cp("act" if ft % 2 == 0 else "pool", c[:, :], s[:, 0:2048], [s], [c])
                k.dma("pool", wdn_s[:, ft // 16, :, ft % 16, :].rearrange("a p c -> p a c"),
                      c[:, :].rearrange("p (a c) -> p a c", a=8), [c], [WDNS.reg(ft, ft + 1)], c)
            steps.append(f)
        return steps

    PSTEPS = precast_steps()
    pstep_i = [0]

    def pump(n):
        if dbg.get("no_precast"):
            return
        for _ in range(n):
            if pstep_i[0] < len(PSTEPS):
                PSTEPS[pstep_i[0]]()
                pstep_i[0] += 1

    def load_w(dst, src, ncols):
        g = 2056 // ncols
        g = 4 if g >= 4 else 2
        srcv = src.rearrange("(kt p) c -> p kt c", p=128)
        for k0 in range(0, KT, g):
            s = stg()
            k.dma("sp", s[:, 0:g * ncols].rearrange("p (a c) -> p a c", a=g), srcv[:, k0:k0 + g, :], [], [s], s)
            for j in range(g):
                kt = k0 + j
                k.ts("dve", dst[:, kt, :], s[:, j * ncols:(j + 1) * ncols], c_attn(kt), ALU.mult,
                     [s, PAR], [dst.reg(kt * ncols, (kt + 1) * ncols)])

    def rsqrt(out, in_, scale, eps, tmp, r, w):
        k.act(tmp, in_, AF.Ln, r, [tmp_b[0]], scale=scale, bias=eps)
        k.act(out, tmp, AF.Exp, [tmp_b[0]], w, scale=-0.5)

    tmp_b = [None]

    dbg_n = [0]

    def dbg_dump(name, src_ap, src_buf, sl=None):
        if name in dbg_out:
            dst = dbg_out[name] if sl is None else dbg_out[name][sl]
            dbg_n[0] += 1
            k.dma("sp", dst, src_ap, [src_buf], [DBG.reg(dbg_n[0], dbg_n[0] + 1)], Buf(f"dbgk{dbg_n[0]}"))

    def rsq(out_ap, in_ap, scale, eps, tbuf, tmp_ap, r, w):
        k.act(tmp_ap, in_ap, AF.Ln, r, [tbuf], scale=scale, bias=eps)
        k.act(out_ap, tmp_ap, AF.Exp, [tbuf], w, scale=-0.5)

    xT_v = xT_d.rearrange("(kt p) t -> p kt t", p=128)

    ar.mark()
    SQ0 = [ar.alloc(f"sq0_{i}", [128, 512], BF16) for i in range(3)]
    LNT = ar.alloc("lnt", [128, 512])
    sqi = 0
    castq = 0
    for tb in range(NTB):
        ssb = psn()
        for kg in range(4):
            s = stg()
            k.dma("sp", s[:, 0:2048].rearrange("p (a c) -> p a c", a=4),
                  xT_v[:, kg * 4:(kg + 1) * 4, tb * TB:(tb + 1) * TB], [], [s], s)
            for j in range(4):
                kt = kg * 4 + j
                sq = SQ0[sqi % 3]; sqi += 1
                k.act(sq[:, :], s[:, j * 512:(j + 1) * 512], AF.Square, [s], [sq])
                k.mm(ssb[:, :], ONESB[:, :], sq[:, :], kt == 0, kt == KT - 1, [ONESB, sq], [ssb])
            c = cbf()
            k.cp("dve" if castq % 2 == 0 else "pool", c[:, :], s[:, 0:2048], [s], [c]); castq += 1
            k.dma("pool", xbf_d[:, kg * 4:(kg + 1) * 4, tb * TB:(tb + 1) * TB],
                  c[:, :].rearrange("p (a c) -> p a c", a=4), [c], [XBF.reg(tb * 4 + kg, tb * 4 + kg + 1)], c)
        rsq(RSTD[:, tb * TB:(tb + 1) * TB], ssb[:, :], 1.0 / D, EPS6, LNT, LNT[:, :], [ssb],
            [RSTD.reg(tb * TB, (tb + 1) * TB)])
    ar.release()
    psr[0] = list(range(8))
    dbg_dump("d_rstd", RSTD[0:1, :], RSTD)
    if dbg.get("stop") == "s0":
        return finish(nc, P, st)

    ar.mark()
    WDN = ar.alloc("wdn", [128, KT, 514], BF16)
    XB = [ar.alloc(f"xb{i}", [128, KT, TB], BF16) for i in range(2)]
    PJ = [ar.alloc(f"pj{s}", [128, 515]) for s in range(3)]
    PZ = ar.alloc("pz", [128, 512])
    BA = ar.alloc("ba", [128, 512])
    CV = [ar.alloc(f"cv{s}", [128, 512]) for s in range(3)]
    TA = ar.alloc("ta", [128, 512])
    TP = ar.alloc("tp", [128, 512])
    SQB = ar.alloc("sqb", [128, 512], BF16)
    RI = ar.alloc("ri", [128, 512])
    dbl = lambda nm, shp: [ar.alloc(f"{nm}{i}", shp) for i in range(2)]
    QN2 = dbl("qn", [128, 512]); KN2 = dbl("kn", [128, 512])
    KBG2 = dbl("kbg", [128, 512]); KDEC2 = dbl("kdec", [128, 512]); VB2 = dbl("vb", [128, 512])
    GSL = ar.alloc("gsl", [128, 512]); GU = ar.alloc("gu", [128, 512])
    EE2 = dbl("ee", [128, 512]); ET2 = dbl("et", [128, 512]); EGCB = ar.alloc("egcb", [128, 512])
    QG2 = dbl("qg", [128, 512]); QKT = ar.alloc("qkt", [128, 512])
    SZ22 = dbl("sz2_", [128, 512]); GT2 = dbl("gates_", [128, 96])
    RI_B = ar.alloc("ri_b", [128, 512]); TA_B = ar.alloc("ta_b", [128, 512]); SQB_B = ar.alloc("sqb_b", [128, 512], BF16)
    XX = [ar.alloc(f"xx{i}", [128, 512]) for i in range(2)]
    XXT = [ar.alloc(f"xxt{i}", [128, 512]) for i in range(2)]
    RR = ar.alloc("rr", [128, 512])
    UU = ar.alloc("uu", [128, 512]); WT = ar.alloc("wt", [128, 512])
    VN = [ar.alloc(f"vn{i}", [128, 128]) for i in range(2)]
    SS = ar.alloc("state", [128, 128])
    OO = ar.alloc("oo", [128, 512])
    YB = [ar.alloc(f"yb{i}", [128, 512], BF16) for i in range(2)]
    print("[arena] DN sweep uses", ar.off, "of", ar.n)
    QSCALE = 0.5 * (128 ** -0.5)
    blk = 0
    psr[0] = list(range(7))
    for h in range(0 if dbg.get("skip_dn") else 4):
        load_w(WDN, wdn_d[h], 514)
        k.memset("pool", SS[:, :], 0.0, [SS])
        for s in range(3):
            k.memset("pool", PJ[s][:, 0:3], 0.0, [PJ[s].reg(0, 3)])
        def partA(tb):
            psr[0] = [0, 1, 2]
            blk = 0
            QN = QN2[tb % 2]; KN = KN2[tb % 2]; KBG = KBG2[tb % 2]; KDEC = KDEC2[tb % 2]; VBt = VB2[tb % 2]
            EE = EE2[tb % 2]; ET = ET2[tb % 2]; QG = QG2[tb % 2]; SZ2 = SZ22[tb % 2]; GT = GT2[tb % 2]
            g_ = lambda a, b: GT[:, a:b]
            gr = lambda a, b: GT.reg(a, b)
            xb = XB[tb % 2]
            k.dma("sp", xb[:, :, :], xbf_d[:, :, tb * TB:(tb + 1) * TB], [XBF.reg(tb * 4, tb * 4 + 4)], [xb], xb)
            pump(dbg.get("pump_dn", 3))
            rblk = RSTD[:, tb * TB:(tb + 1) * TB]
            rreg = RSTD.reg(tb * TB, (tb + 1) * TB)
            for ct in range(4):
                ps = psn()
                for kt in range(KT):
                    k.mm(ps[:, :], WDN[:, kt, ct * 128:(ct + 1) * 128], xb[:, kt, :], kt == 0, kt == KT - 1, [WDN, xb], [ps])
                if ct < 3:
                    k.tt("dve", PJ[ct][:, 3:515], ps[:, :], rblk, ALU.mult, [ps, rreg], [PJ[ct].reg(3, 515)])
                else:
                    k.tt("dve", PZ[:, :], ps[:, :], rblk, ALU.mult, [ps, rreg], [PZ])
            ps = psn()
            for kt in range(KT):
                k.mm(ps[0:2, :], WDN[:, kt, 512:514], xb[:, kt, :], kt == 0, kt == KT - 1, [WDN, xb], [ps])
            k.tt("dve", BA[0:2, :], ps[0:2, :], RSTD[0:2, tb * TB:(tb + 1) * TB], ALU.mult, [ps, rreg], [BA])
            psg = psn()
            for c in range(4):
                k.mm(psg[:, c * 2:c * 2 + 2], BA[0:2, c * 128:(c + 1) * 128], ident[0:2, 0:2], True, True, [BA, CON], [psg.reg(c * 2, c * 2 + 2)])
            k.cp("dve", g_(0, 8), psg[:, 0:8], [psg], [gr(0, 8)])
            raw = GT[:, 0:8].rearrange("p (c t) -> p c t", t=2)
            bcol = raw[:, :, 0]
            acol = raw[:, :, 1]
            k.act(g_(8, 12), bcol, AF.Exp, [gr(0, 8)], [gr(8, 12)], scale=-1.0)
            k.ts("dve", g_(8, 12), g_(8, 12), 1.0, ALU.add, [gr(8, 12)], [gr(8, 12)])
            k.recip(g_(8, 12), g_(8, 12), [gr(8, 12)], [gr(8, 12)])
            k.ts("dve", g_(12, 16), acol, c_dtb(h), ALU.add, [gr(0, 8), PAR], [gr(12, 16)])
            k.act(g_(16, 20), g_(12, 16), AF.Abs, [gr(12, 16)], [gr(16, 20)])
            k.act(g_(20, 24), g_(16, 20), AF.Exp, [gr(16, 20)], [gr(20, 24)], scale=-1.0)
            k.act(g_(20, 24), g_(20, 24), AF.Ln, [gr(20, 24)], [gr(20, 24)], bias=1.0)
            k.ts("dve", g_(24, 28), g_(12, 16), 0.0, ALU.max, [gr(12, 16)], [gr(24, 28)])
            k.tt("dve", g_(24, 28), g_(24, 28), g_(20, 24), ALU.add, [gr(24, 28), gr(20, 24)], [gr(24, 28)])
            k.ts("dve", g_(28, 32), g_(24, 28), SM[:, h:h + 1], ALU.mult, [gr(24, 28), SM.reg(0, 4)], [gr(28, 32)])
            for c in range(4):
                cs = slice(c * 128, (c + 1) * 128)
                k.ts("dve", GSL[:, cs], SLm, g_(28 + c, 29 + c), ALU.mult, [CON, gr(28, 32)], [GSL.reg(c * 128, (c + 1) * 128)])
                k.ts("dve", GU[:, cs], Um, g_(28 + c, 29 + c), ALU.mult, [CON, gr(28, 32)], [GU.reg(c * 128, (c + 1) * 128)])
            psGC = psn(); psGCC = psn()
            k.mm(psGC[:, :], ones, GU[:, :], True, True, [CON, GU], [psGC])
            k.mm(psGCC[:, :], Um, GU[:, :], True, True, [CON, GU], [psGCC])
            gccol = psGCC[:, :].rearrange("p (c i) -> p c i", i=128)[:, :, 127]
            glbc = psGC[:, :].rearrange("p (c i) -> p c i", i=128)[:, :, 127]
            k.cp("dve", g_(32, 36), gccol, [psGCC], [gr(32, 36)])
            k.act(g_(36, 40), gccol, AF.Exp, [psGCC], [gr(36, 40)])
            k.tt("dve", g_(40, 44), glbc, g_(32, 36), ALU.subtract, [psGC, gr(32, 36)], [gr(40, 44)])
            k.act(g_(44, 48), g_(40, 44), AF.Exp, [gr(40, 44)], [gr(44, 48)])
            k.act(g_(48, 52), glbc, AF.Exp, [psGC], [gr(48, 52)])
            k.act(EGCB[:, :], psGC[:, :], AF.Exp, [psGC], [EGCB])
            k.tt("dve", g_(52, 56), g_(8, 12), g_(36, 40), ALU.mult, [gr(8, 12), gr(36, 40)], [gr(52, 56)])
            k.ts("dve", g_(56, 60), g_(8, 12), 0.5, ALU.mult, [gr(8, 12)], [gr(56, 60)])
            k.ts("dve", g_(60, 64), g_(8, 12), -1.0, ALU.mult, [gr(8, 12)], [gr(60, 64)])
            for s in range(3):
                eng = "dve"
                pj = PJ[s]; cv = CV[s]
                k.ts(eng, cv[:, :], pj[:, 0:512], c_conv(h, s, 0), ALU.mult, [pj, PAR], [cv])
                for j in range(1, 4):
                    k.stt(eng, cv[:, :], pj[:, j:j + 512], c_conv(h, s, j), cv[:, :], ALU.mult, ALU.add, [pj, PAR, cv], [cv],
                          tmp=TP[:, :], tmpb=TP)
                k.cp("pool", pj[:, 0:3], pj[:, 512:515], [pj.reg(512, 515)], [pj.reg(0, 3)])
                k.act(TA[:, :], cv[:, :], AF.Tanh, [cv], [TA], scale=0.5)
                k.stt("dve", cv[:, :], TA[:, :], 1.0, cv[:, :], ALU.add, ALU.mult, [TA, cv], [cv])
            k.act(TA[:, :], PZ[:, :], AF.Tanh, [PZ], [TA], scale=0.5)
            k.stt("dve", SZ2[:, :], TA[:, :], 1.0, PZ[:, :], ALU.add, ALU.mult, [TA, PZ], [SZ2])
            for s, dst, sc in ((0, QN, QSCALE), (1, KN, 0.5)):
                k.act(SQB[:, :], CV[s][:, :], AF.Square, [CV[s]], [SQB])
                ps = psn()
                k.mm(ps[:, :], ONESB[:, :], SQB[:, :], True, True, [ONESB, SQB], [ps])
                rsq(RI[:, :], ps[:, :], 0.25, EPS6, TA, TA[:, :], [ps], [RI])
                k.stt("dve", dst[:, :], CV[s][:, :], sc, RI[:, :], ALU.mult, ALU.mult, [CV[s], RI], [dst])
            psk = psn(); psv = psn()
            for c in range(4):
                cs = slice(c * 128, (c + 1) * 128)
                k.tr(psk[:, cs], KN[:, cs], ident, [KN, CON], [psk.reg(c * 128, (c + 1) * 128)])
                k.tr(psv[:, cs], CV[2][:, cs], ident, [CV[2], CON], [psv.reg(c * 128, (c + 1) * 128)])
            for c in range(4):
                cs = slice(c * 128, (c + 1) * 128)
                k.ts("dve", KBG[:, cs], psk[:, cs], g_(52 + c, 53 + c), ALU.mult, [psk, gr(52, 56)], [KBG.reg(c * 128, (c + 1) * 128)])
                k.ts("dve", KDEC[:, cs], psk[:, cs], g_(44 + c, 45 + c), ALU.mult, [psk, gr(44, 48)], [KDEC.reg(c * 128, (c + 1) * 128)])
                k.ts("dve", VBt[:, cs], psv[:, cs], g_(56 + c, 57 + c), ALU.mult, [psv, gr(56, 60)], [VBt.reg(c * 128, (c + 1) * 128)])
            psD = psn(); psDT = psn()
            k.mm(psD[:, :], Um, GSL[:, :], True, True, [CON, GSL], [psD])
            k.mm(psDT[:, :], SLm, GU[:, :], True, True, [CON, GU], [psDT])
            k.act(EE[:, :], psD[:, :], AF.Exp, [psD], [EE])
            k.tt("pool", EE[:, :], EE[:, :], strict4, ALU.mult, [EE, CON], [EE])
            k.act(ET[:, :], psDT[:, :], AF.Exp, [psDT], [ET])
            k.tt("pool", ET[:, :], ET[:, :], lowT4, ALU.mult, [ET, CON], [ET])
            k.tt("pool", QG[:, :], QN[:, :], EGCB[:, :], ALU.mult, [QN, EGCB], [QG])

        def partB(tb):
            psr[0] = [3, 4, 5, 6]
            QN = QN2[tb % 2]; KN = KN2[tb % 2]; KBG = KBG2[tb % 2]; KDEC = KDEC2[tb % 2]; VBt = VB2[tb % 2]
            EE = EE2[tb % 2]; ET = ET2[tb % 2]; QG = QG2[tb % 2]; SZ2 = SZ22[tb % 2]; GT = GT2[tb % 2]
            g_ = lambda a, b: GT[:, a:b]
            gr = lambda a, b: GT.reg(a, b)
            RI = RI_B; TA = TA_B; SQB = SQB_B
            psKK = psn(); psKQ = psn()
            for c in range(4):
                cs = slice(c * 128, (c + 1) * 128)
                k.mm(psKK[:, cs], KN[:, cs], KN[:, cs], True, True, [KN], [psKK.reg(c * 128, (c + 1) * 128)])
                k.mm(psKQ[:, cs], KN[:, cs], QN[:, cs], True, True, [KN, QN], [psKQ.reg(c * 128, (c + 1) * 128)])
            X0 = XX[0]; XT0 = XXT[0]
            for c in range(4):
                cs = slice(c * 128, (c + 1) * 128)
                k.stt("dve", X0[:, cs], psKK[:, cs], g_(60 + c, 61 + c), EE[:, cs], ALU.mult, ALU.mult,
                      [psKK, gr(60, 64), EE], [X0.reg(c * 128, (c + 1) * 128)])
            k.tt("dve", QKT[:, :], psKQ[:, :], ET[:, :], ALU.mult, [psKQ, ET], [QKT])
            psT = psn()
            for c in range(4):
                cs = slice(c * 128, (c + 1) * 128)
                k.tr(psT[:, cs], X0[:, cs], ident, [X0, CON], [psT.reg(c * 128, (c + 1) * 128)])
            k.cp("act", XT0[:, :], psT[:, :], [psT], [XT0])
            k.tt("dve", RR[:, :], psT[:, :], ident4, ALU.add, [psT, CON], [RR])
            cur = 0
            for n in range(1, 7):
                Xc, XTc = XX[cur], XXT[cur]
                Xn, XTn = XX[1 - cur], XXT[1 - cur]
                psX = psn()
                for c in range(4):
                    cs = slice(c * 128, (c + 1) * 128)
                    k.mm(psX[:, cs], XTc[:, cs], Xc[:, cs], True, True, [XTc, Xc], [psX.reg(c * 128, (c + 1) * 128)])
                if n < 6:
                    psXT = psn()
                    for c in range(4):
                        cs = slice(c * 128, (c + 1) * 128)
                        k.mm(psXT[:, cs], Xc[:, cs], XTc[:, cs], True, True, [XTc, Xc], [psXT.reg(c * 128, (c + 1) * 128)])
                k.cp("act", Xn[:, :], psX[:, :], [psX], [Xn])
                if n < 6:
                    k.cp("dve", XTn[:, :], psXT[:, :], [psXT], [XTn])
                psR = psn()
                for c in range(4):
                    cs = slice(c * 128, (c + 1) * 128)
                    k.mm(psR[:, cs], Xn[:, cs], RR[:, cs], True, True, [Xn, RR], [psR.reg(c * 128, (c + 1) * 128)])
                k.tt("dve", RR[:, :], psR[:, :], RR[:, :], ALU.add, [psR, RR], [RR])
                cur = 1 - cur
            psU = psn(); psW = psn()
            for c in range(4):
                cs = slice(c * 128, (c + 1) * 128)
                k.mm(psU[:, cs], RR[:, cs], VBt[:, cs], True, True, [RR, VBt], [psU.reg(c * 128, (c + 1) * 128)])
                k.mm(psW[:, cs], KBG[:, cs], RR[:, cs], True, True, [RR, KBG], [psW.reg(c * 128, (c + 1) * 128)])
            k.cp("act", UU[:, :], psU[:, :], [psU], [UU])
            k.cp("dve", WT[:, :], psW[:, :], [psW], [WT])
            psO = PS[7]
            for c in range(4):
                cs = slice(c * 128, (c + 1) * 128)
                creg = (c * 128, (c + 1) * 128)
                ps1 = psn()
                k.mm(ps1[:, 0:128], WT[:, cs], SS[:, :], True, True, [WT, SS], [ps1])
                k.mm(psO[:, cs], SS[:, :], QG[:, cs], True, False, [SS, QG], [psO.reg(*creg)])
                vn = VN[c % 2]
                k.tt("dve", vn[:, :], UU[:, cs], ps1[:, 0:128], ALU.subtract, [UU, ps1], [vn])
                k.mm(psO[:, cs], vn[:, :], QKT[:, cs], False, True, [vn, QKT], [psO.reg(*creg)])
                ps3 = psn()
                k.mm(ps3[:, 0:128], KDEC[:, cs], vn[:, :], True, True, [KDEC, vn], [ps3])
                k.stt("dve", SS[:, :], SS[:, :], g_(48 + c, 49 + c), ps3[:, 0:128], ALU.mult, ALU.add, [SS, gr(48, 52), ps3], [SS])
            k.cp("act", OO[:, :], psO[:, :], [psO], [OO])
            k.act(SQB[:, :], psO[:, :], AF.Square, [psO], [SQB])
            ps = psn()
            k.mm(ps[:, :], ONESB[:, :], SQB[:, :], True, True, [ONESB, SQB], [ps])
            rsq(RI[:, :], ps[:, :], 1.0 / 128, EPS6, TA, TA[:, :], [ps], [RI])
            k.tt("pool", OO[:, :], OO[:, :], RI[:, :], ALU.mult, [OO, RI], [OO])
            yb = YB[tb % 2]
            k.stt("dve", yb[:, :], OO[:, :], SM[:, 4:5], SZ2[:, :], ALU.mult, ALU.mult, [OO, SM.reg(4, 5), SZ2], [yb])
            k.dma("sp", send_l[tb][h * 128:(h + 1) * 128, :], yb[:, :], [yb], [SENDB[tb].reg(h, h + 1)], yb)
            if h == dbg.get("dh", 0) and tb == dbg.get("dtb", 0):
                dbg_dump("d_qn", QN[:, :], QN); dbg_dump("d_kn", KN[:, :], KN); dbg_dump("d_v2", CV[2][:, :], CV[2])
                dbg_dump("d_gates", GT[:, 0:64], GT); dbg_dump("d_tt", RR[:, :], RR); dbg_dump("d_oo", OO[:, :], OO)
                dbg_dump("d_uu", UU[:, :], UU); dbg_dump("d_wt", WT[:, :], WT); dbg_dump("d_qkt", QKT[:, :], QKT)

        def record(fn, tb):
            lst = []
            P.add = lambda *a, **kw: lst.append((a, kw))
            try:
                fn(tb)
            finally:
                del P.add
            return lst

        def flush_merged(la, lb):
            ia = ib = 0
            na, nb = len(la), len(lb)
            while ia < na or ib < nb:
                if ib >= nb or (ia < na and ia * nb <= ib * na):
                    a, kw = la[ia]; ia += 1
                else:
                    a, kw = lb[ib]; ib += 1
                P.add(*a, **kw)

        recA = record(partA, 0)
        flush_merged(recA, [])
        for tb in range(NTB):
            recB = record(partB, tb)
            recA = record(partA, tb + 1) if tb + 1 < NTB else []
            if dbg.get("no_overlap"):
                flush_merged(recB, []); flush_merged(recA, [])
            else:
                flush_merged(recB, recA)
            if dbg.get("stop") == "dn00" or (dbg.get("stop") == "dn0" and tb == NTB - 1):
                return finish(nc, P, st)
    ar.release()
    psr[0] = list(range(8))

    rg = [[0, 1], [2, 3], [4, 5], [6, 7]]

    def exchange(tb):
        src = send_l[tb]; dst = recv_l[tb]
        P.add("pool", lambda e: e.collective_compute("AllGather", ALU.bypass, replica_groups=rg, ins=[src], outs=[dst]),
              reads=[SENDB[tb]], writes=[RECVB[tb]], dma=RECVB[tb], inc=1)

    ar.mark()
    WDF = ar.alloc("wdf", [128, KT, 768], BF16)
    XB = [ar.alloc(f"xbd{i}", [128, KT, TB], BF16) for i in range(2)]
    KC = [ar.alloc(f"kc{m}", [128, T], BF16) for m in range(2)]
    VC = ar.alloc("vc", [128, 32, 256], BF16)
    BT = ar.alloc("bt", [128, 5, 512])
    QB = [ar.alloc(f"qb{m}", [128, 512], BF16) for m in range(2)]
    RCOL = ar.alloc("rcol", [128, 4])
    PT = [ar.alloc(f"pt{i}", [128, 512], BF16) for i in range(4)]
    LG = [ar.alloc(f"lg{m}", [128, 512]) for m in range(2)]
    RL = [ar.alloc(f"rl{m}", [128, 512]) for m in range(2)]
    T1 = ar.alloc("t1", [128, 512])
    OD = [ar.alloc(f"od{i}", [128, 512]) for i in range(2)]
    SQD = [ar.alloc(f"sqd{i}", [128, 512], BF16) for i in range(2)]
    RI = ar.alloc("rid", [128, 512]); TA = ar.alloc("tad", [128, 512])
    YD = [ar.alloc(f"yd{i}", [128, 512], BF16) for i in range(4)]
    ASCALE = 128 ** -0.5
    ydi = 0
    blk = 0
    for hl in range(2):
        load_w(WDF, wdf_d[hl], 768)
        k.dma("sp", BT[:, :, :], bias_d[hl].rearrange("t p c -> p t c"), [], [BT], BT)
        for tb in range(NTB):
            xb = XB[blk % 2]; blk += 1
            k.dma("sp", xb[:, :, :], xbf_d[:, :, tb * TB:(tb + 1) * TB], [XBF.reg(tb * 4, tb * 4 + 4)], [xb], xb)
            pump(dbg.get("pump_df", 3))
            rblk = RSTD[:, tb * TB:(tb + 1) * TB]
            rreg = RSTD.reg(tb * TB, (tb + 1) * TB)
            for ct in range(4):
                ps = psn()
                for kt in range(KT):
                    k.mm(ps[:, :], WDF[:, kt, ct * 128:(ct + 1) * 128], xb[:, kt, :], kt == 0, kt == KT - 1, [WDF, xb], [ps])
                if ct < 2:
                    k.tt("dve", QB[ct][:, :], ps[:, :], rblk, ALU.mult, [ps, rreg], [QB[ct]])
                else:
                    m = ct - 2
                    k.tt("dve", KC[m][:, tb * TB:(tb + 1) * TB], ps[:, :], rblk, ALU.mult, [ps, rreg], [KC[m].reg(tb * TB, (tb + 1) * TB)])
            psc = psn()
            for t4 in range(4):
                k.mm(psc[:, t4:t4 + 1], RSTD[0:1, tb * TB + t4 * 128:tb * TB + (t4 + 1) * 128], ones[0:1, 0:1], True, True,
                     [rreg, CON], [psc.reg(t4, t4 + 1)])
            k.cp("dve", RCOL[:, :], psc[:, 0:4], [psc], [RCOL])
            for t4 in range(4):
                ps = psn()
                for kt in range(KT):
                    k.mm(ps[:, 0:256], xb[:, kt, t4 * 128:(t4 + 1) * 128], WDF[:, kt, 512:768], kt == 0, kt == KT - 1, [WDF, xb], [ps])
                tile_i = tb * 4 + t4
                k.ts("dve", VC[:, tile_i, :], ps[:, 0:256], RCOL[:, t4:t4 + 1], ALU.mult, [ps, RCOL], [VC.reg(tile_i * 256, (tile_i + 1) * 256)])
            nkt = 4 * tb + 4
            ACC = [[PS[m * 3 + j] for j in range(3)] for m in range(2)]
            SB_ = [PS[6], PS[7]]

            def s_and_e(kt, m):
                rel = tb * TB - kt * 128
                k.mm(SB_[m][:, :], KC[m][:, kt * 128:(kt + 1) * 128], QB[m][:, :], True, True,
                     [KC[m].reg(kt * 128, (kt + 1) * 128), QB[m]], [SB_[m]])
                pt = PT[(kt * 2 + m) % 4]
                if rel >= 256:
                    k.act(pt[:, :], SB_[m][:, :], AF.Exp, [SB_[m], PAR], [pt], scale=ASCALE, bias=c_c31(hl))
                else:
                    typ = 0 if rel == 128 else 1 + (-rel) // 128
                    k.stt("dve", LG[m][:, :], SB_[m][:, :], ASCALE, BT[:, typ, :], ALU.mult, ALU.add, [SB_[m], BT], [LG[m]])
                    k.act(pt[:, :], LG[m][:, :], AF.Exp, [LG[m]], [pt])

            def pv(kt, m):
                pt = PT[(kt * 2 + m) % 4]
                st_, sp_ = (kt == 0), (kt == nkt - 1)
                for dvc in range(2):
                    k.mm(ACC[m][dvc][:, :], VC[:, kt, dvc * 128:(dvc + 1) * 128], pt[:, :], st_, sp_,
                         [VC.reg(kt * 256, (kt + 1) * 256), pt], [ACC[m][dvc]])
                k.mm(ACC[m][2][:, :], ONESB[:, :], pt[:, :], st_, sp_, [ONESB, pt], [ACC[m][2]])

            s_and_e(0, 0); s_and_e(0, 1)
            for kt in range(nkt):
                if kt + 1 < nkt:
                    s_and_e(kt + 1, 0)
                pv(kt, 0)
                if kt + 1 < nkt:
                    s_and_e(kt + 1, 1)
                pv(kt, 1)
            for m in range(2):
                k.recip(RL[m][:, :], ACC[m][2][:, :], [ACC[m][2]], [RL[m]])
            k.ts("dve", RL[1][:, :], RL[1][:, :], SM[:, 8:9], ALU.mult, [RL[1], SM.reg(8, 9)], [RL[1]])
            for dvc in range(2):
                k.tt("dve", T1[:, :], ACC[0][dvc][:, :], RL[0][:, :], ALU.mult, [ACC[0][dvc], RL[0]], [T1])
                k.tt("dve", OD[dvc][:, :], ACC[1][dvc][:, :], RL[1][:, :], ALU.mult, [ACC[1][dvc], RL[1]], [OD[dvc]])
                k.tt("pool", OD[dvc][:, :], OD[dvc][:, :], T1[:, :], ALU.add, [OD[dvc], T1], [OD[dvc]])
            ps = PS[6]
            for dvc in range(2):
                k.act(SQD[dvc][:, :], OD[dvc][:, :], AF.Square, [OD[dvc]], [SQD[dvc]])
                k.mm(ps[:, :], ONESB[:, :], SQD[dvc][:, :], dvc == 0, dvc == 1, [ONESB, SQD[dvc]], [ps])
            rsq(RI[:, :], ps[:, :], 1.0 / 256, 1e-5, TA, TA[:, :], [ps], [RI])
            for dvc in range(2):
                yb = YD[ydi % 4]; ydi += 1
                k.stt("dve", yb[:, :], OD[dvc][:, :], SM[:, 6 + dvc:7 + dvc], RI[:, :], ALU.mult, ALU.mult,
                      [OD[dvc], SM.reg(6, 8), RI], [yb])
                row = (4 + hl * 2 + dvc) * 128
                rid = 4 + hl * 2 + dvc
                k.dma("sp", send_l[tb][row:row + 128, :], yb[:, :], [yb], [SENDB[tb].reg(rid, rid + 1)], yb)
            if hl == 1 and not dbg.get("no_cc"):
                exchange(tb)
            if hl == dbg.get("dh", 0) and tb == dbg.get("dtb", 1):
                dbg_dump("d_od0", OD[0][:, :], OD[0]); dbg_dump("d_od1", OD[1][:, :], OD[1])
            if dbg.get("stop") == "df01" and tb == dbg.get("dtb", 1):
                return finish(nc, P, st)
    ar.release()
    pump(10 ** 6)
    ar.release()
    if dbg.get("stop") == "mix":
        return finish(nc, P, st)

    MIX = ar.alloc("mix", [128, KT, TB], BF16)
    H1 = ar.alloc("h1", [128, KT, TB])
    U2 = ar.alloc("u2", [128, KT, TB], BF16)
    HID = ar.alloc("hid", [128, 64, TB], BF16)
    WSL = [ar.alloc(f"wsl{i}", [128, 16, 256], BF16) for i in range(4)]
    RA = [ar.alloc(f"ra{i}", [128, 512], BF16) for i in range(2)]
    RBb = [ar.alloc(f"rb{i}", [128, 512], BF16) for i in range(2)]
    SQP = [ar.alloc(f"sqp{i}", [128, 512], BF16) for i in range(2)]
    RI = ar.alloc("ri3", [128, 512]); TA = ar.alloc("ta3", [128, 512])
    RT = [ar.alloc(f"rt{i}", [128, 512]) for i in range(2)]
    OB = [ar.alloc(f"ob{i}", [128, 512]) for i in range(2)]
    wsi = [0]

    def wslot():
        b = WSL[wsi[0] % 4]
        wsi[0] += 1
        return b

    xTo_v = xTo_d.rearrange("(kt p) t -> p kt t", p=128)
    sq_i = 0
    for ob in range(TOWN // TB):
        tsl = slice(ob * TB, (ob + 1) * TB)
        for kg in range(4):
            k.dma("sp", H1[:, kg * 4:(kg + 1) * 4, :], xTo_v[:, kg * 4:(kg + 1) * 4, tsl], [], [H1.reg(kg * 2048, (kg + 1) * 2048)], H1)
        for rt in range(16):
            ra = RA[rt % 2]; rb = RBb[rt % 2]
            k.dma("sp", ra[:, :], recv_l[ob][rt * 128:(rt + 1) * 128, :], [RECVB[ob]], [ra], ra)
            k.dma("sp", rb[:, :], recv_l[4 + ob][rt * 128:(rt + 1) * 128, :], [RECVB[4 + ob]], [rb], rb)
            mreg = MIX.reg(rt * TB, (rt + 1) * TB)
            k.ts("dve", MIX[:, rt, :], ra[:, :], c_sel(0), ALU.mult, [ra, PAR], [mreg])
            k.stt("dve", MIX[:, rt, :], rb[:, :], c_sel(1), MIX[:, rt, :], ALU.mult, ALU.add, [rb, PAR, mreg], [mreg])
        for dt2 in range(8):
            sl = wslot()
            k.dma("sp", sl[:, :, :], wo_s[dt2], [WOS], [sl], sl)
            for half in range(2):
                dt = dt2 * 2 + half
                ps = psn()
                for kt in range(KT):
                    k.mm(ps[:, :], sl[:, kt, half * 128:(half + 1) * 128], MIX[:, kt, :], kt == 0, kt == KT - 1, [sl, MIX], [ps])
                hreg = H1.reg(dt * TB, (dt + 1) * TB)
                k.tt("dve", H1[:, dt, :], ps[:, :], H1[:, dt, :], ALU.add, [ps, hreg], [hreg])
        if ob == 0:
            dbg_dump("d_h1", H1[:, 0, :], H1)
        ps = psn()
        for dt in range(KT):
            sq = SQP[sq_i % 2]; sq_i += 1
            k.act(sq[:, :], H1[:, dt, :], AF.Square, [H1.reg(dt * TB, (dt + 1) * TB)], [sq])
            k.mm(ps[:, :], ONESB[:, :], sq[:, :], dt == 0, dt == KT - 1, [ONESB, sq], [ps])
        rsq(RI[:, :], ps[:, :], 1.0 / D, EPS6, TA, TA[:, :], [ps], [RI])
        for dt in range(KT):
            k.tt("dve" if dt % 2 == 0 else "pool", U2[:, dt, :], H1[:, dt, :], RI[:, :], ALU.mult,
                 [H1.reg(dt * TB, (dt + 1) * TB), RI], [U2.reg(dt * TB, (dt + 1) * TB)])
        for ft2 in range(32):
            sl = wslot()
            k.dma("sp", sl[:, :, :], wup_s[ft2], [WUPS], [sl], sl)
            for half in range(2):
                ft = ft2 * 2 + half
                ps = psn()
                for kt in range(KT):
                    k.mm(ps[:, :], sl[:, kt, half * 128:(half + 1) * 128], U2[:, kt, :], kt == 0, kt == KT - 1, [sl, U2], [ps])
                rt_ = RT[ft % 2]
                k.act(rt_[:, :], ps[:, :], AF.Relu, [ps], [rt_])
                k.tt("pool" if ft % 4 != 3 else "dve", HID[:, ft, :], rt_[:, :], rt_[:, :], ALU.mult, [rt_], [HID.reg(ft * TB, (ft + 1) * TB)])
        for dt2 in range(8):
            psA = psn(); psB = psn()
            pss = [psA, psB]
            for fg in range(4):
                sl = wslot()
                k.dma("sp", sl[:, :, :], wdn_s[dt2, fg], [WDNS], [sl], sl)
                for half in range(2):
                    for f in range(16):
                        ft = fg * 16 + f
                        k.mm(pss[half][:, :], sl[:, f, half * 128:(half + 1) * 128], HID[:, ft, :], fg == 0 and f == 0, fg == 3 and f == 15,
                             [sl, HID.reg(ft * TB, (ft + 1) * TB)], [pss[half]])
            for half in range(2):
                dt = dt2 * 2 + half
                hreg = H1.reg(dt * TB, (dt + 1) * TB)
                k.tt("dve", H1[:, dt, :], pss[half][:, :], H1[:, dt, :], ALU.add, [pss[half], hreg], [hreg])
        ps = psn()
        for dt in range(KT):
            sq = SQP[sq_i % 2]; sq_i += 1
            k.act(sq[:, :], H1[:, dt, :], AF.Square, [H1.reg(dt * TB, (dt + 1) * TB)], [sq])
            k.mm(ps[:, :], ONESB[:, :], sq[:, :], dt == 0, dt == KT - 1, [ONESB, sq], [ps])
        rsq(RI[:, :], ps[:, :], 1.0 / D, EPS6, TA, TA[:, :], [ps], [RI])
        for dt in range(KT):
            ob_ = OB[dt % 2]
            k.stt("dve", ob_[:, :], H1[:, dt, :], c_fin(dt), RI[:, :], ALU.mult, ALU.mult,
                  [H1.reg(dt * TB, (dt + 1) * TB), PAR, RI], [ob_])
            k.dma("sp", out_d[dt * 128:(dt + 1) * 128, tsl], ob_[:, :], [ob_], [OUTD.reg(ob * 16 + dt, ob * 16 + dt + 1)], ob_)
    return finish(nc, P, st)


def finish(nc, P, st):
    P.emit(nc, st)
    st.close()
    return nc


def _bucket(n):
    nf = np.maximum(n, 1).astype(np.float32)
    large = 16 + (np.log(nf / np.float32(16)) / np.float32(math.log(128 / 16)) * np.float32(16)).astype(np.int32)
    large = np.minimum(large, 31)
    return np.where(n < 16, n, large)


def _consts():
    i = np.arange(128)
    strict = (i[:, None] > i[None, :]).astype(np.float32)
    lowT = (i[:, None] <= i[None, :]).astype(np.float32)
    ident = np.eye(128, dtype=np.float32)
    c = np.concatenate([np.tile(strict, (1, 4)), np.tile(lowT, (1, 4)), np.tile(ident, (1, 4)),
                        np.ones((128, 128), np.float32)], axis=1)
    return np.ascontiguousarray(c)


def prep_inputs(inputs, cores=range(8)):
    x = np.asarray(inputs["x"], np.float32)
    w_in = np.asarray(inputs["w_in"], np.float32)[0]
    conv_w = np.asarray(inputs["conv_w"], np.float32)[0]
    a_log = np.asarray(inputs["a_log"], np.float32)[0]
    dt_bias = np.asarray(inputs["dt_bias"], np.float32)[0]
    rel_bias = np.asarray(inputs["rel_bias"], np.float32)
    w_o = np.asarray(inputs["w_o"], np.float32)[0]
    w_up = np.ascontiguousarray(np.asarray(inputs["w_up"], np.float32)[0])
    w_down = np.ascontiguousarray(np.asarray(inputs["w_down"], np.float32)[0])
    consts = _consts()
    p = np.arange(128)
    perm = []
    for r in range(2):
        for hh_ in range(4):
            perm.append((r * 4 + hh_) * 128 + p)
        for hl in range(2):
            for dvc in range(2):
                perm.append(1024 + (r * 2 + hl) * 256 + dvc * 128 + p)
    perm = np.concatenate(perm)
    w_o_perm = np.ascontiguousarray(w_o[perm, :])
    qq = np.arange(512)[None, :]
    kk = np.arange(128)[:, None]
    xT_cache = {}
    maps = []
    for c in cores:
        b, hh = c // 2, c % 2
        if b not in xT_cache:
            xT_cache[b] = np.ascontiguousarray(x[b].T)
        xT = xT_cache[b]
        xTo = np.ascontiguousarray(xT[:, hh * TOWN:(hh + 1) * TOWN])
        w_dn = np.empty((4, D, 514), np.float32)
        for h in range(4):
            H = hh * 4 + h
            for s in range(4):
                w_dn[h, :, s * 128:(s + 1) * 128] = w_in[:, s * 1024 + H * 128:s * 1024 + (H + 1) * 128]
            w_dn[h, :, 512] = w_in[:, 4096 + H]
            w_dn[h, :, 513] = w_in[:, 4104 + H]
        w_df = np.empty((2, D, 768), np.float32)
        for hl in range(2):
            Hf = hh * 2 + hl
            base = 4112
            w_df[hl, :, 0:256] = w_in[:, base + Hf * 256:base + (Hf + 1) * 256]
            w_df[hl, :, 256:512] = w_in[:, base + 1024 + Hf * 256:base + 1024 + (Hf + 1) * 256]
            w_df[hl, :, 512:768] = w_in[:, base + 2048 + Hf * 256:base + 2048 + (Hf + 1) * 256]
        par = np.zeros((128, NPAR), np.float32)
        par[:, 0:16] = np.asarray(inputs["attn_norm"], np.float32)[0].reshape(16, 128).T
        par[:, 16:32] = np.asarray(inputs["mlp_norm"], np.float32)[0].reshape(16, 128).T
        par[:, 32:48] = np.asarray(inputs["final_norm"], np.float32).reshape(16, 128).T
        for h in range(4):
            H = hh * 4 + h
            for s in range(3):
                for j in range(4):
                    par[:, 48 + (h * 3 + s) * 4 + j] = conv_w[j, s * 1024 + H * 128:s * 1024 + (H + 1) * 128]
            par[:, 96 + h] = a_log[H]
            par[:, 100 + h] = dt_bias[H]
        par[:, 104] = np.asarray(inputs["dn_norm"], np.float32)[0]
        par[:, 105] = np.asarray(inputs["lambda_q1"], np.float32)[0]
        par[:, 106] = np.asarray(inputs["lambda_k1"], np.float32)[0]
        par[:, 107] = np.asarray(inputs["lambda_q2"], np.float32)[0]
        par[:, 108] = np.asarray(inputs["lambda_k2"], np.float32)[0]
        par[:, 109:111] = np.asarray(inputs["df_norm"], np.float32)[0].reshape(2, 128).T
        bias_t = np.empty((2, 5, 128, 512), np.float32)
        for hl in range(2):
            Hf = hh * 2 + hl
            par[:, 111 + hl] = rel_bias[31, Hf]
            for typ in range(5):
                n = (qq + 128 - kk) if typ == 0 else (qq - kk - 128 * (typ - 1))
                tbl = rel_bias[_bucket(np.maximum(n, 0)), Hf]
                bias_t[hl, typ] = np.where(n >= 0, tbl, np.float32(-30000.0))
        par[:, 113] = 1.0 if hh == 0 else 0.0
        par[:, 114] = 0.0 if hh == 0 else 1.0
        maps.append({"xT": xT, "xTo": xTo, "w_dn": w_dn, "w_df": w_df, "w_o": w_o_perm, "w_up": w_up, "w_down": w_down,
                     "consts": consts, "par": par, "bias_t": bias_t})
    return maps


_NC_CACHE = {}


def kernel(**inputs):
    from concourse.bass_utils import run_bass_kernel_spmd
    if "nc" not in _NC_CACHE:
        _NC_CACHE["nc"] = build()
    nc = _NC_CACHE["nc"]
    maps = prep_inputs(inputs)
    res = run_bass_kernel_spmd(nc, maps, core_ids=list(range(8)))
    out = np.empty((4, T, D), np.float32)
    for c in range(8):
        b, hh = c // 2, c % 2
        out[b, hh * TOWN:(hh + 1) * TOWN, :] = np.asarray(res.results[c]["outT"]).T
    return out
```
